# Optimizing a Trainium2 kernel written in Bass

```python
import jax, jax.numpy as jnp
from jax import lax
import numpy as np

D_MODEL = 1024
BATCH = 16
SEQ = 2048
DEPTH = 1
DEC_BATCH = 32
DEC_SEQ = 8
PAST_LEN = 16384
PAGE_SIZE = 128

HEAD_DIM = 64
D_MIX = D_MODEL
MEM_HEADS = 4
FOX_HEADS = (D_MIX // HEAD_DIM - MEM_HEADS) // 2
SB_HEADS = D_MIX // HEAD_DIM - MEM_HEADS - FOX_HEADS
FOX_W = FOX_HEADS * HEAD_DIM
SB_W = SB_HEADS * HEAD_DIM
MEM_W = MEM_HEADS * HEAD_DIM
N_IN = 3 * FOX_W + FOX_HEADS + 3 * SB_W + MEM_W
N_MEM = 256
D_FF = 4 * D_MODEL
Q_BLOCK = 128
EPS = 1e-6
FORGET_BIAS_INIT = 8.0

kernel_name = "fox_stickbreak_memory_hybrid_step"


def _rmsnorm(x, g=None):
    xf = x.astype(jnp.float32)
    y = xf * lax.rsqrt(jnp.mean(xf * xf, axis=-1, keepdims=True) + EPS)
    if g is not None:
        y = y * g.astype(jnp.float32)
    return y.astype(x.dtype)


def _project(h, w_in, b_f, g_fq, g_fk, g_mq):
    B, T, _ = h.shape
    p = jnp.einsum('btd,dn->btn', h, w_in)
    cuts = [int(c) for c in np.cumsum([FOX_W, FOX_W, FOX_W, FOX_HEADS, SB_W, SB_W, SB_W])]
    fq, fk, fv, fg, sq, sk, sv, mq = jnp.split(p, cuts, axis=-1)
    heads = lambda a, n: a.reshape(B, T, n, HEAD_DIM)
    fq = _rmsnorm(heads(fq, FOX_HEADS), g_fq)
    fk = _rmsnorm(heads(fk, FOX_HEADS), g_fk)
    fv = heads(fv, FOX_HEADS)
    logf = jax.nn.log_sigmoid((fg + b_f).astype(jnp.float32))
    sq, sk, sv = heads(sq, SB_HEADS), heads(sk, SB_HEADS), heads(sv, SB_HEADS)
    mq = _rmsnorm(heads(mq, MEM_HEADS), g_mq)
    return fq, fk, fv, logf, sq, sk, sv, mq


def _memory_kv(mem, g_mem, w_mem_kv, g_mk):
    B = mem.shape[0]
    kv = jnp.einsum('bmd,dn->bmn', _rmsnorm(mem, g_mem), w_mem_kv)
    mk, mv = jnp.split(kv, 2, axis=-1)
    mk = _rmsnorm(mk.reshape(B, N_MEM, MEM_HEADS, HEAD_DIM), g_mk)
    mv = mv.reshape(B, N_MEM, MEM_HEADS, HEAD_DIM)
    return mk, mv


def _fox_attend(q, k, v, cq, ck, qpos, kpos):
    s = jnp.einsum('bqhd,bkhd->bhqk', q, k, preferred_element_type=jnp.float32) * (HEAD_DIM ** -0.5)
    s = s + jnp.transpose(cq, (0, 2, 1))[..., None] - jnp.transpose(ck, (0, 2, 1))[:, :, None, :]
    mask = kpos[None, :] <= qpos[:, None]
    p = jax.nn.softmax(jnp.where(mask, s, -jnp.inf), axis=-1)
    return jnp.einsum('bhqk,bkhd->bqhd', p.astype(v.dtype), v)


def _sb_attend(q, k, v, qpos, kpos):
    z = jnp.einsum('bqhd,bkhd->bhqk', q, k, preferred_element_type=jnp.float32) * (HEAD_DIM ** -0.5)
    mask = kpos[None, :] < qpos[:, None]
    log_keep = jnp.where(mask, jax.nn.log_sigmoid(-z), 0.0)
    later = lax.cumsum(log_keep, axis=3, reverse=True) - log_keep
    a = jnp.where(mask, jnp.exp(jax.nn.log_sigmoid(z) + later), 0.0)
    return jnp.einsum('bhqk,bkhd->bqhd', a.astype(v.dtype), v)


def _mem_attend(q, mk, mv):
    s = jnp.einsum('bqhd,bmhd->bhqm', q, mk, preferred_element_type=jnp.float32) * (HEAD_DIM ** -0.5)
    p = jax.nn.softmax(s, axis=-1)
    return jnp.einsum('bhqm,bmhd->bqhd', p.astype(mv.dtype), mv)


def _to_blocks(a):
    B, S = a.shape[:2]
    return jnp.moveaxis(a.reshape((B, S // Q_BLOCK, Q_BLOCK) + a.shape[2:]), 1, 0)


def _from_blocks(a):
    nb, B = a.shape[:2]
    return jnp.moveaxis(a, 0, 1).reshape((B, nb * Q_BLOCK) + a.shape[3:])


def _merge(x, fo, so, mo, g_out, w_out):
    B, T, _ = x.shape
    o = _rmsnorm(jnp.concatenate([fo, so, mo], axis=2)).reshape(B, T, D_MIX)
    o = (o * g_out).astype(x.dtype)
    return x + jnp.einsum('btc,cd->btd', o, w_out)


def _ffn(x, g_ffn, w_up, w_down):
    u = jnp.einsum('btd,df->btf', _rmsnorm(x, g_ffn), w_up)
    return x + jnp.einsum('btf,fd->btd', jnp.square(jax.nn.relu(u)), w_down)


def setup_inputs(seed: int = 0) -> dict:
    key = jax.random.key(seed)
    ks = jax.random.split(key, 32)
    n_pages = PAST_LEN // PAGE_SIZE
    n_used = DEC_BATCH * n_pages
    n_pool = n_used + n_used // 4
    nrm = lambda k, shape, scale=1.0: jax.random.normal(k, shape, jnp.float32) * scale
    gain = lambda k, shape: 1.0 + 0.1 * jax.random.normal(k, shape, jnp.float32)
    kv_pool = (DEPTH, n_pool, PAGE_SIZE, FOX_HEADS, HEAD_DIM)
    sb_pool = (DEPTH, n_pool, PAGE_SIZE, SB_HEADS, HEAD_DIM)
    mem_cache = (DEPTH, DEC_BATCH, N_MEM, MEM_HEADS, HEAD_DIM)
    page_table = jax.random.permutation(ks[10], n_pool)[:n_used].reshape(DEC_BATCH, n_pages).astype(jnp.int32)
    return {
        "x_prompt": nrm(ks[0], (BATCH, SEQ, D_MODEL)),
        "x_sample": nrm(ks[1], (DEC_BATCH, DEC_SEQ, D_MODEL)),
        "mem_prompt": nrm(ks[2], (BATCH, N_MEM, D_MODEL)),
        "cache_fox_k": nrm(ks[3], kv_pool),
        "cache_fox_v": nrm(ks[4], kv_pool),
        "cache_fox_logf": jax.nn.log_sigmoid(FORGET_BIAS_INIT + nrm(ks[5], (DEPTH, n_pool, PAGE_SIZE, FOX_HEADS))),
        "cache_sb_k": nrm(ks[6], sb_pool),
        "cache_sb_v": nrm(ks[7], sb_pool),
        "cache_mem_k": nrm(ks[8], mem_cache),
        "cache_mem_v": nrm(ks[9], mem_cache),
        "page_table": page_table,
        "g_attn": gain(ks[11], (DEPTH, D_MODEL)),
        "w_in": nrm(ks[12], (DEPTH, D_MODEL, N_IN), D_MODEL ** -0.5),
        "b_forget": FORGET_BIAS_INIT + nrm(ks[13], (DEPTH, FOX_HEADS), 0.1),
        "g_fox_q": gain(ks[14], (DEPTH, HEAD_DIM)),
        "g_fox_k": gain(ks[15], (DEPTH, HEAD_DIM)),
        "g_mem_in": gain(ks[16], (DEPTH, D_MODEL)),
        "w_mem_kv": nrm(ks[17], (DEPTH, D_MODEL, 2 * MEM_W), D_MODEL ** -0.5),
        "g_mem_q": gain(ks[18], (DEPTH, HEAD_DIM)),
        "g_mem_k": gain(ks[19], (DEPTH, HEAD_DIM)),
        "g_out": gain(ks[20], (DEPTH, D_MIX)),
        "w_out": nrm(ks[21], (DEPTH, D_MIX, D_MODEL), D_MIX ** -0.5),
        "g_ffn": gain(ks[22], (DEPTH, D_MODEL)),
        "w_up": nrm(ks[23], (DEPTH, D_MODEL, D_FF), D_MODEL ** -0.5),
        "w_down": nrm(ks[24], (DEPTH, D_FF, D_MODEL), D_FF ** -0.5),
    }


def reference(x_prompt, x_sample, mem_prompt, cache_fox_k, cache_fox_v, cache_fox_logf, cache_sb_k, cache_sb_v,
              cache_mem_k, cache_mem_v, page_table, g_attn, w_in, b_forget, g_fox_q, g_fox_k, g_mem_in, w_mem_kv,
              g_mem_q, g_mem_k, g_out, w_out, g_ffn, w_up, w_down):
    yp, ys = x_prompt, x_sample
    Bp, S, _ = x_prompt.shape
    Bs, T, _ = x_sample.shape
    past = page_table.shape[1] * PAGE_SIZE
    pos_p = jnp.arange(S)
    qpos_s = past + jnp.arange(T)
    kpos_s = jnp.arange(past + T)
    p_fk, p_fv, p_lf, p_sk, p_sv, p_mk, p_mv = [], [], [], [], [], [], []
    s_fk, s_fv, s_lf, s_sk, s_sv = [], [], [], [], []
    for l in range(DEPTH):
        h = _rmsnorm(yp, g_attn[l])
        fq, fk, fv, logf, sq, sk, sv, mq = _project(h, w_in[l], b_forget[l], g_fox_q[l], g_fox_k[l], g_mem_q[l])
        mk, mv = _memory_kv(mem_prompt, g_mem_in[l], w_mem_kv[l], g_mem_k[l])
        c = jnp.cumsum(logf, axis=1)
        fo = _from_blocks(lax.map(
            lambda blk: _fox_attend(blk[0], fk, fv, blk[1], c, blk[2], pos_p),
            (_to_blocks(fq), _to_blocks(c), pos_p.reshape(-1, Q_BLOCK))))
        so = _from_blocks(lax.map(
            lambda blk: _sb_attend(blk[0], sk, sv, blk[1], pos_p),
            (_to_blocks(sq), pos_p.reshape(-1, Q_BLOCK))))
        mo = _mem_attend(mq, mk, mv)
        yp = _ffn(_merge(yp, fo, so, mo, g_out[l], w_out[l]), g_ffn[l], w_up[l], w_down[l])
        p_fk.append(fk); p_fv.append(fv); p_lf.append(logf); p_sk.append(sk); p_sv.append(sv)
        p_mk.append(mk); p_mv.append(mv)

        h = _rmsnorm(ys, g_attn[l])
        fq, fk, fv, logf, sq, sk, sv, mq = _project(h, w_in[l], b_forget[l], g_fox_q[l], g_fox_k[l], g_mem_q[l])
        gather = lambda pool: pool[page_table].reshape((Bs, past) + pool.shape[2:])
        fk_all = jnp.concatenate([gather(cache_fox_k[l]), fk.astype(cache_fox_k.dtype)], axis=1)
        fv_all = jnp.concatenate([gather(cache_fox_v[l]), fv.astype(cache_fox_v.dtype)], axis=1)
        lp = gather(cache_fox_logf[l]).astype(jnp.float32)
        ck_past = lp - lax.cumsum(lp, axis=1, reverse=True)
        cnew = jnp.cumsum(logf, axis=1)
        ck = jnp.concatenate([ck_past, cnew], axis=1)
        fo = _fox_attend(fq, fk_all, fv_all, cnew, ck, qpos_s, kpos_s)
        sk_all = jnp.concatenate([gather(cache_sb_k[l]), sk.astype(cache_sb_k.dtype)], axis=1)
        sv_all = jnp.concatenate([gather(cache_sb_v[l]), sv.astype(cache_sb_v.dtype)], axis=1)
        so = _sb_attend(sq, sk_all, sv_all, qpos_s, kpos_s)
        mo = _mem_attend(mq, cache_mem_k[l], cache_mem_v[l])
        ys = _ffn(_merge(ys, fo, so, mo, g_out[l], w_out[l]), g_ffn[l], w_up[l], w_down[l])
        s_fk.append(fk); s_fv.append(fv); s_lf.append(logf); s_sk.append(sk); s_sv.append(sv)
    st = jnp.stack
    return (yp, ys, st(p_fk), st(p_fv), st(p_lf), st(p_sk), st(p_sv), st(p_mk), st(p_mv),
            st(s_fk), st(s_fv), st(s_lf), st(s_sk), st(s_sv))
```

```python
from contextlib import ExitStack
import numpy as np
import concourse.bass as bass
import concourse.mybir as mybir
from concourse.bass_utils import run_bass_kernel_spmd

F32 = mybir.dt.float32
BF16 = mybir.dt.bfloat16
I32 = mybir.dt.int32
AF = mybir.ActivationFunctionType
ALU = mybir.AluOpType
AX = mybir.AxisListType
EPS = 1e-6
NEG = -30000.0
D = 1024
NIN = 2566
C_FQ, C_FK, C_FV, C_FG, C_SQ, C_SK, C_SV, C_MQ = 0, 384, 768, 1152, 1158, 1542, 1926, 2310


class Cfg:
    def __init__(self, NSEQ=2, S=2048, NPG=128, NPOOL=5120, NBL=4, T=8, DFF=4096):
        self.NSEQ, self.S, self.NPG, self.NPOOL, self.NBL, self.T, self.DFF = NSEQ, S, NPG, NPOOL, NBL, T, DFF


class Buf:
    __slots__ = ("name", "w", "r", "excl")

    def __init__(self, name, excl=False):
        self.name = name
        self.w = None
        self.r = {}
        self.excl = excl


class Sched:
    EPOCH = 30000

    def __init__(self, nc, es, n_dma_sems=32):
        self.nc = nc
        self.es = es
        self.engs = {"pe": nc.tensor, "act": nc.scalar, "dve": nc.vector, "pool": nc.gpsimd, "sp": nc.sync}
        self.cur = {}
        self.cnt = {}
        self.nsem = 0
        self.waited = {e: {} for e in self.engs}
        for e in self.engs:
            self._new_epoch(e)
        self.dma_sems = [[self._sem("d%d" % i), 0] for i in range(n_dma_sems)]
        self.dma_rr = 0
        self.last_ev = {e: None for e in self.engs}
        self.all_dma_evs = []

    def _sem(self, name):
        self.nsem += 1
        s = self.es.enter_context(self.nc.semaphore("s%d_%s" % (self.nsem, name)))
        return (s, self.nsem)

    def _new_epoch(self, e):
        self.cur[e] = self._sem(e)
        self.cnt[e] = 0

    def wait(self, e, ev):
        if ev is None:
            return
        sem, val, key = ev
        if self.waited[e].get(key, 0) >= val:
            return
        if e in ("pe", "sp") and key == self.cur[e][1]:
            return
        self.engs[e].wait_ge(sem, val)
        self.waited[e][key] = val

    def _deps(self, e, reads, writes):
        for b in reads:
            self.wait(e, b.w)
        for b in writes:
            self.wait(e, b.w)
            for ev in b.r.values():
                self.wait(e, ev)

    def _commit(self, ev, reads, writes):
        for b in reads:
            b.r[ev[2]] = ev
        for b in writes:
            b.w = ev
            b.r = {}

    def op(self, e, fn, reads=(), writes=()):
        if self.cnt[e] >= self.EPOCH:
            self._new_epoch(e)
        writes = list(writes) + [b for b in reads if b.excl and b not in writes]
        reads = [b for b in reads if not b.excl]
        self._deps(e, reads, writes)
        inst = fn(self.engs[e])
        sem, key = self.cur[e]
        self.cnt[e] += 1
        inst.then_inc(sem, 1)
        ev = (sem, self.cnt[e], key)
        self.last_ev[e] = ev
        self._commit(ev, reads, writes)
        return ev

    def dma(self, q, fn, reads=(), writes=(), group=None):
        self._deps(q, reads, writes)
        if group is not None and group:
            slot = group[0]
        else:
            slot = self.dma_sems[self.dma_rr % len(self.dma_sems)]
            self.dma_rr += 1
            (sem, key), val = slot
            self.wait(q, (sem, val, key))
            if group is not None:
                group.append(slot)
        (sem, key), val = slot
        inst = fn(self.engs[q])
        inst.then_inc(sem, 16)
        slot[1] = val + 16
        ev = (sem, val + 16, key)
        self._commit(ev, reads, writes)
        self.all_dma_evs.append(ev)
        if len(self.all_dma_evs) > 256:
            self._compact()
        return ev

    def _compact(self):
        devs = {}
        for ev in self.all_dma_evs:
            if ev[2] not in devs or devs[ev[2]][1] < ev[1]:
                devs[ev[2]] = ev
        self.all_dma_evs = list(devs.values())

    def barrier(self):
        self._compact()
        evs = [ev for ev in self.last_ev.values() if ev is not None]
        for e in self.engs:
            for ev in evs:
                self.wait(e, ev)
            for ev in self.all_dma_evs:
                self.wait(e, ev)

    def finish(self, e="sp"):
        self._compact()
        for ev in self.all_dma_evs:
            self.wait(e, ev)
        for ev in self.last_ev.values():
            self.wait(e, ev)


class _Stop(Exception):
    pass


class T_:
    def __init__(self, t, name):
        self.t = t
        self.b = Buf(name)

    def __getitem__(self, k):
        return self.t[k]


def build(cfg):
    NSEQ, S, NPG, NPOOL, NBL, T, DFF = cfg.NSEQ, cfg.S, cfg.NPG, cfg.NPOOL, cfg.NBL, cfg.T, cfg.DFF
    NT = S // 128
    NR = S // 512
    NTS = NBL * T
    NCS = 6 * NTS
    NFC = DFF // 128
    nc = bass.Bass("TRN2", target_bir_lowering=False)

    def din(name, shape, dt=F32):
        return nc.dram_tensor(name, list(shape), dt, kind="ExternalInput").ap()

    def dout(name, shape, dt=F32):
        return nc.dram_tensor(name, list(shape), dt, kind="ExternalOutput").ap()

    xp = din("xp", [NSEQ * S, D]); xs = din("xs", [NTS, D]); memp = din("memp", [NSEQ * 256, D])
    fkT = din("fkT", [NPOOL * 64, 768]); fv = din("fv", [NPOOL * 128, 384]); flf = din("flf", [NPOOL * 128, 6])
    skT = din("skT", [NPOOL * 64, 768]); sv = din("sv", [NPOOL * 128, 384])
    cmkT = din("cmkT", [NBL, 64, 4, 256]); cmv = din("cmv", [NBL, 256, 256])
    ptd = din("pt", [1, NBL * NPG], I32)
    g_attn = din("g_attn", [1, D]); w_in = din("w_in", [D, NIN]); b_f = din("b_f", [1, 6])
    g_fq = din("g_fq", [1, 64]); g_fk = din("g_fk", [1, 64]); g_mem_in = din("g_mem_in", [1, D])
    w_mem = din("w_mem", [D, 512]); g_mq = din("g_mq", [1, 64]); g_mk = din("g_mk", [1, 64])
    g_out = din("g_out", [1, D]); w_out = din("w_out", [D, D]); g_ffn = din("g_ffn", [1, D])
    w_up = din("w_up", [D, DFF]); w_down = din("w_down", [DFF, D])
    yp = dout("yp", [NSEQ * S, D]); ys = dout("ys", [NTS, D])
    p_fk = dout("p_fk", [NSEQ * S, 384]); p_fv = dout("p_fv", [NSEQ * S, 384]); p_lf = dout("p_lf", [NSEQ * S, 6])
    p_sk = dout("p_sk", [NSEQ * S, 384]); p_sv = dout("p_sv", [NSEQ * S, 384])
    p_mk = dout("p_mk", [NSEQ * 256, 256]); p_mv = dout("p_mv", [NSEQ * 256, 256])
    s_fk = dout("s_fk", [NTS, 384]); s_fv = dout("s_fv", [NTS, 384]); s_lf = dout("s_lf", [NTS, 6])
    s_sk = dout("s_sk", [NTS, 384]); s_sv = dout("s_sv", [NTS, 384])
    xmid = nc.dram_tensor("xmid", [NSEQ * S + 128, D], F32, kind="Internal").ap()

    with ExitStack() as es:
        sc = Sched(nc, es)

        esA = ExitStack()
        dflt = [es]

        def sbt(name, shape, dt, st=None):
            return T_((st or dflt[0]).enter_context(nc.sbuf_tensor(name, list(shape), dt)), name)

        def pst(name):
            t = T_(es.enter_context(nc.psum_tensor(name, [128, 512], F32)), name)
            t.b.excl = True
            return t

        def bl(ts):
            return [t.b for t in ts]

        def ACT(out, in_, func, r, w, **kw):
            return sc.op("act", lambda e: e.activation(out=out, in_=in_, func=func, **kw), bl(r), bl(w))

        def MM(out, lhsT, rhs, start, stop, r, w):
            return sc.op("pe", lambda e: e.matmul(out, lhsT=lhsT, rhs=rhs, start=start, stop=stop), bl(r), bl(w))

        def TR(out, in_, ident, r, w):
            return sc.op("pe", lambda e: e.transpose(out=out, in_=in_, identity=ident), bl(r), bl(w))

        def V(fn, r, w, *a, **kw):
            return sc.op("dve", lambda e: getattr(e, fn)(*a, **kw), bl(r), bl(w))

        def G(fn, r, w, *a, **kw):
            return sc.op("pool", lambda e: getattr(e, fn)(*a, **kw), bl(r), bl(w))

        def DMA(out, in_, r, w, q="sp", **kw):
            return sc.dma(q, lambda e: e.dma_start(out=out, in_=in_, **kw), bl(r), bl(w))

        def bcast_row(ap_row, n):
            return ap_row.rearrange("o n -> (o n)").partition_broadcast(128)

        A = [pst("A0"), pst("A1")]
        Bk = [pst("B0"), pst("B1")]
        O = [pst("O0"), pst("O1")]
        Tk = [pst("T0"), pst("T1")]
        ctr = {"A": 0, "B": 0, "O": 0, "T": 0}

        def nxt(lst, k):
            ctr[k] += 1
            return lst[ctr[k] % len(lst)]

        ident_f = sbt("ident_f", [128, 128], F32); ident_b = sbt("ident_b", [128, 128], BF16)
        gX = sbt("gX", [128, D], F32)
        junk = sbt("junk", [128, D], BF16); hb = sbt("hb", [128, D], BF16)
        ss = sbt("ss", [128, 4], F32)
        hT = [sbt("hT0", [128, 8, 128], BF16), sbt("hT1", [128, 8, 128], BF16)]
        ptmp = [sbt("ptmp0", [128, 512], F32), sbt("ptmp1", [128, 512], F32)]
        wst = [sbt("wst0", [128, 1280], F32), sbt("wst1", [128, 1280], F32)]
        dflt[0] = esA
        triL = sbt("triL", [128, 128], F32); ones_f = sbt("ones_f", [128, 128], F32)
        triU = sbt("triU", [128, 128], F32)
        triSB = sbt("triSB", [128, 128], BF16); negones = sbt("negones", [128, 128], BF16)
        onecol = sbt("onecol", [128, 1], BF16)
        G("memset", [], [ident_f], ident_f[:], 1.0)
        G("affine_select", [ident_f], [ident_f], out=ident_f[:], in_=ident_f[:], pattern=[[-1, 128]],
          compare_op=ALU.is_equal, fill=0.0, base=0, channel_multiplier=1)
        G("tensor_copy", [ident_f], [ident_b], out=ident_b[:], in_=ident_f[:])
        G("memset", [], [ones_f], ones_f[:], 1.0)
        G("memset", [], [triL], triL[:], 1.0)
        G("affine_select", [triL], [triL], out=triL[:], in_=triL[:], pattern=[[1, 128]],
          compare_op=ALU.is_ge, fill=0.0, base=0, channel_multiplier=-1)
        G("memset", [], [triU], triU[:], 1.0)
        G("affine_select", [triU], [triU], out=triU[:], in_=triU[:], pattern=[[-1, 128]],
          compare_op=ALU.is_gt, fill=0.0, base=0, channel_multiplier=1)
        G("memset", [], [triSB], triSB[:], -1.0)
        G("affine_select", [triSB], [triSB], out=triSB[:], in_=triSB[:], pattern=[[-1, 128]],
          compare_op=ALU.is_ge, fill=0.0, base=0, channel_multiplier=1)
        G("memset", [], [negones], negones[:], -1.0)
        G("memset", [], [onecol], onecol[:], 1.0)
        mfox = sbt("mfox", [128, 4, 512], BF16); msb = sbt("msb", [128, 4, 512], BF16)
        for i in range(4):
            for (m, op) in ((mfox, ALU.is_ge), (msb, ALU.is_gt)):
                G("memset", [], [m], m[:, i, :], 0.0)
                G("affine_select", [m], [m], out=m[:, i, :], in_=m[:, i, :], pattern=[[1, 512]],
                  compare_op=op, fill=NEG, base=-128 * i, channel_multiplier=-1)
        mnf = sbt("mnf", [128, NCS], BF16); mns = sbt("mns", [128, NCS], BF16)
        for (m, op) in ((mnf, ALU.is_ge), (mns, ALU.is_gt)):
            G("memset", [], [m], m[:], 0.0)
            G("affine_select", [m], [m], out=m[:], in_=m[:], pattern=[[0, 6], [T, NBL], [1, T]],
              compare_op=op, fill=NEG, base=0, channel_multiplier=-1)
            G("affine_select", [m], [m], out=m[:], in_=m[:], pattern=[[0, 6], [-T, NBL], [0, T]],
              compare_op=ALU.is_ge, fill=NEG, base=0, channel_multiplier=1)
        Mrev = sbt("Mrev", [128, 128], F32)
        G("memset", [], [Mrev], Mrev[:], 1.0)
        G("affine_select", [Mrev], [Mrev], out=Mrev[:], in_=Mrev[:], pattern=[[-1, 128]],
          compare_op=ALU.is_gt, fill=0.0, base=0, channel_multiplier=1)
        G("affine_select", [Mrev], [Mrev], out=Mrev[:], in_=Mrev[:], pattern=[[T, 128 // T], [0, T]],
          compare_op=ALU.is_gt, fill=0.0, base=T, channel_multiplier=-1)
        selB = sbt("selB", [128, NBL, 128], F32)
        G("memset", [], [selB], selB[:], 1.0)
        for b in range(NBL):
            G("affine_select", [selB], [selB], out=selB[:, b, :], in_=selB[:, b, :], pattern=[[0, 128]],
              compare_op=ALU.is_ge, fill=0.0, base=-T * b, channel_multiplier=1)
            G("affine_select", [selB], [selB], out=selB[:, b, :], in_=selB[:, b, :], pattern=[[0, 128]],
              compare_op=ALU.is_ge, fill=0.0, base=T * b + T - 1, channel_multiplier=-1)
        gA = sbt("gA", [128, D], F32); gO = sbt("gO", [128, D], F32)
        DMA(gA[:], bcast_row(g_attn, D), [], [gA]); DMA(gO[:], bcast_row(g_out, D), [], [gO])
        gfq8 = sbt("gfq8", [128, 64], F32); gfk = sbt("gfk", [128, 64], F32)
        gmq8 = sbt("gmq8", [128, 64], F32); gmk = sbt("gmk", [128, 64], F32); bfb = sbt("bfb", [128, 6], F32)
        DMA(gfq8[:], bcast_row(g_fq, 64), [], [gfq8]); DMA(gfk[:], bcast_row(g_fk, 64), [], [gfk])
        DMA(gmq8[:], bcast_row(g_mq, 64), [], [gmq8]); DMA(gmk[:], bcast_row(g_mk, 64), [], [gmk])
        DMA(bfb[:], bcast_row(b_f, 6), [], [bfb])
        V("tensor_scalar", [gfq8], [gfq8], out=gfq8[:], in0=gfq8[:], scalar1=0.125, scalar2=None, op0=ALU.mult)
        V("tensor_scalar", [gmq8], [gmq8], out=gmq8[:], in0=gmq8[:], scalar1=0.125, scalar2=None, op0=ALU.mult)

        xt = [sbt("xt0", [128, D], F32), sbt("xt1", [128, D], F32)]
        hsq = sbt("hsq", [128, 512], F32); hs = sbt("hs", [128, 48], F32); hn2 = sbt("hn2", [128, 512], F32)
        pbf = [sbt("pbf0", [128, 512], BF16), sbt("pbf1", [128, 512], BF16)]
        PT = [sbt("PT0", [128, 512], BF16), sbt("PT1", [128, 512], BF16)]
        spb = [sbt("spb0", [128, 512], BF16), sbt("spb1", [128, 512], BF16)]
        etmp = sbt("etmp", [128, 512], F32)
        etmps = [etmp, sbt("etmp1", [128, 512], F32)]
        raccf = sbt("raccf", [128, 512], F32); raccb = sbt("raccb", [128, 512], BF16)
        OTs = sbt("OTs", [128, 512], F32); otok = sbt("otok", [128, 4, 64], F32)
        lft = sbt("lft", [128, 24], F32); lfo = sbt("lfo", [128, 6], F32)
        carry = sbt("carry", [128, 6], F32); cc = sbt("cc", [128, 6], F32); chi = sbt("chi", [128, 6], BF16)
        clo = sbt("clo", [128, 6], F32)
        qaug = sbt("qaug", [128, 6, 68], BF16); kaug = sbt("kaug", [128, 6, 68], BF16)
        wbuf = sbt("wbuf", [128, 8, 1158], BF16)
        k = {"x": 0, "h": 0, "p": 0, "pb": 0, "PT": 0, "sp": 0, "e": 0}

        def rr(lst, key):
            k[key] += 1
            return lst[k[key] % len(lst)]

        G("memset", [], [qaug], qaug[:], 1.0)
        G("memset", [], [kaug], kaug[:], 1.0)

        def load_w(dram, r0_unused, c0, n, dst=None):
            dst = dst or wbuf
            for kc in range(8):
                stg = wst[kc % 2]
                DMA(stg[:, 0:n], dram[kc * 128:(kc + 1) * 128, c0:c0 + n], [], [stg])
                G("tensor_copy", [stg], [dst], out=dst[:, kc, 0:n], in_=stg[:, 0:n])

        def norm_T(x_ap, rows, gain, src_t=None):
            if src_t is None:
                x = rr(xt, "x")
                if rows < 128:
                    G("memset", [], [x], x[:], 0.0)
                DMA(x[0:rows, :], x_ap, [], [x])
            else:
                x = src_t
            ACT(junk[:], x[:], AF.Square, [x], [junk, ss], accum_out=ss[:, 0:1])
            ACT(ss[:, 1:2], ss[:, 0:1], AF.Ln, [ss], [ss], scale=1.0 / D, bias=EPS)
            ACT(ss[:, 2:3], ss[:, 1:2], AF.Exp, [ss], [ss], scale=-0.5)
            V("scalar_tensor_tensor", [x, ss, gain], [hb], out=hb[:], in0=x[:], scalar=ss[:, 2:3], in1=gain[:],
              op0=ALU.mult, op1=ALU.mult)
            tb = nxt(Tk, "T")
            tbv = tb[:].bitcast(BF16)
            for c in range(8):
                TR(tbv[:, c * 128:(c + 1) * 128], hb[:, c * 128:(c + 1) * 128], ident_b[:], [hb, ident_b], [tb])
            h = rr(hT, "h")
            V("tensor_copy", [tb], [h], out=h[:].rearrange("p c t -> p (c t)"), in_=tbv[:, 0:1024])
            return h, x

        def proj(h, w, c0, n, bank):
            for kc in range(8):
                MM(bank[:, 0:n], h[:, kc, :], w[:, kc, c0:c0 + n], kc == 0, kc == 7, [h, w], [bank])

        def headnorm(src3, srcT, H, gain3, outs):
            n = H * 64
            sq3 = hsq[:, 0:n].rearrange("p (h d) -> p h d", h=H)
            ACT(sq3, src3, AF.Square, [srcT], [hsq])
            V("reduce_sum", [hsq], [hs], out=hs[:, 0:H], in_=sq3, axis=AX.X)
            ACT(hs[:, 16:16 + H], hs[:, 0:H], AF.Ln, [hs], [hs], scale=1.0 / 64, bias=EPS)
            ACT(hs[:, 32:32 + H], hs[:, 16:16 + H], AF.Exp, [hs], [hs], scale=-0.5)
            t3 = hn2[:, 0:n].rearrange("p (h d) -> p h d", h=H)
            V("tensor_tensor", [srcT, hs], [hn2], out=t3, in0=src3,
              in1=hs[:, 32:32 + H].unsqueeze(2).to_broadcast([128, H, 64]), op=ALU.mult)
            for (o3, oT) in outs:
                V("tensor_tensor", [hn2], [oT], out=o3, in0=t3, in1=gain3, op=ALU.mult)

        def g3(gt, H):
            return gt[:, :].unsqueeze(1).to_broadcast([128, H, 64])

        def logsig(src, srcT):
            V("tensor_tensor", [srcT, bfb], [lft], out=lft[:, 0:6], in0=src, in1=bfb[:], op=ALU.add)
            ACT(lft[:, 6:12], lft[:, 0:6], AF.Exp, [lft], [lft], scale=-1.0)
            ACT(lft[:, 12:18], lft[:, 6:12], AF.Ln, [lft], [lft], bias=1.0)
            V("tensor_scalar", [lft], [lfo], out=lfo[:], in0=lft[:, 12:18], scalar1=-1.0, scalar2=None, op0=ALU.mult)

        def fox_block(zfill, N, bias3=None, biasT=None):
            a = nxt(A, "A")
            zfill(a)
            pt_ = rr(PT, "PT")
            if bias3 is None:
                ACT(pt_[:, 0:N], a[:, 0:N], AF.Exp, [a], [pt_])
            else:
                tmp = rr(ptmp, "p")
                V("tensor_tensor", [a, biasT], [tmp], out=bias3[0](tmp), in0=bias3[0](a), in1=bias3[1], op=ALU.add)
                ACT(pt_[:, 0:N], tmp[:, 0:N], AF.Exp, [tmp], [pt_])
            return pt_

        def sb_block(zfill, N, first):
            a = nxt(A, "A")
            zfill(a)
            ACT(etmp[:, 0:N], a[:, 0:N], AF.Exp, [a], [etmp])
            sp = rr(spb, "sp")
            ACT(sp[:, 0:N], etmp[:, 0:N], AF.Ln, [etmp], [sp], bias=1.0)
            b2 = nxt(Bk, "B")
            zfill(b2, last=False)
            MM(b2[:, 0:N], triSB[:], sp[:, 0:N], False, first, [triSB, sp], [b2])
            if not first:
                MM(b2[:, 0:N], negones[:], raccb[:, 0:N], False, True, [negones, raccb], [b2])
            pt_ = rr(PT, "PT")
            ACT(pt_[:, 0:N], b2[:, 0:N], AF.Exp, [b2], [pt_])
            if first:
                G("tensor_copy", [sp], [raccf], out=raccf[:, 0:N], in_=sp[:, 0:N])
            else:
                G("tensor_tensor", [sp, raccf], [raccf], out=raccf[:, 0:N], in0=raccf[:, 0:N], in1=sp[:, 0:N], op=ALU.add)
            G("tensor_copy", [raccf], [raccb], out=raccb[:, 0:N], in_=raccf[:, 0:N])
            return pt_


        def fox_attn(nb, zf_k, v_k, Ob, N=512):
            banks = {}

            def qk(i):
                a = nxt(A, "A")
                zf_k(i)(a)
                banks[i] = a
            qk(0)
            if nb > 1:
                qk(1)
            for i in range(nb):
                a = banks.pop(i)
                pt_ = rr(PT, "PT")
                ACT(pt_[:, 0:N], a[:, 0:N], AF.Exp, [a], [pt_])
                if i + 2 < nb:
                    qk(i + 2)
                lhs, lT = v_k(i)
                MM(Ob[0:65, 0:N], lhs, pt_[:, 0:N], i == 0, i == nb - 1, [lT, pt_], [Ob])

        def sb_attn(nb, zf_k, v_k, Ob, N=512):
            abank = {}
            spk = {}

            def qkA(i):
                a = nxt(A, "A")
                zf_k(i)(a)
                abank[i] = a

            def softplus(i):
                a = abank.pop(i)
                e = rr(etmps, "e")
                ACT(e[:, 0:N], a[:, 0:N], AF.Exp, [a], [e])
                sp = rr(spb, "sp")
                ACT(sp[:, 0:N], e[:, 0:N], AF.Ln, [e], [sp], bias=1.0)
                spk[i] = sp
            qkA(0)
            if nb > 1:
                qkA(1)
            softplus(0)
            if nb > 2:
                qkA(2)
            for i in range(nb):
                if i + 1 < nb:
                    softplus(i + 1)
                    if i + 3 < nb:
                        qkA(i + 3)
                sp = spk.pop(i)
                b2 = nxt(Bk, "B")
                zf_k(i)(b2, last=False)
                MM(b2[:, 0:N], triSB[:], sp[:, 0:N], False, i == 0, [triSB, sp], [b2])
                if i > 0:
                    MM(b2[:, 0:N], negones[:], raccb[:, 0:N], False, True, [negones, raccb], [b2])
                pt_ = rr(PT, "PT")
                ACT(pt_[:, 0:N], b2[:, 0:N], AF.Exp, [b2], [pt_])
                lhs, lT = v_k(i)
                MM(Ob[0:65, 0:N], lhs, pt_[:, 0:N], i == 0, i == nb - 1, [lT, pt_], [Ob])
                if i == 0:
                    G("tensor_copy", [sp], [raccf], out=raccf[:, 0:N], in_=sp[:, 0:N])
                else:
                    G("tensor_tensor", [sp, raccf], [raccf], out=raccf[:, 0:N], in0=raccf[:, 0:N], in1=sp[:, 0:N], op=ALU.add)
                if i + 1 < nb:
                    G("tensor_copy", [raccf], [raccb], out=raccb[:, 0:N], in_=raccf[:, 0:N])

        def finish_head(src_ap, srcT, ntile, has_den, hidx, o_n, tile0, copy=True):
            ncol = ntile * 128
            if copy:
                V("tensor_copy", [srcT], [OTs], out=OTs[0:65, 0:ncol], in_=src_ap)
            tb = nxt(Tk, "T")
            for i in range(ntile):
                TR(tb[:, i * 65:(i + 1) * 65], OTs[0:65, i * 128:(i + 1) * 128], ident_f[0:65, 0:65], [OTs, ident_f], [tb])
            t3 = tb[:, 0:ntile * 65].rearrange("p (i c) -> p i c", c=65)
            o3 = otok[:, 0:ntile, :]
            if has_den:
                V("reciprocal", [tb], [hs], out=hs[:, 40:40 + ntile].unsqueeze(2), in_=t3[:, :, 64:65])
                V("tensor_tensor", [tb, hs], [otok], out=o3, in0=t3[:, :, 0:64],
                  in1=hs[:, 40:40 + ntile].unsqueeze(2).to_broadcast([128, ntile, 64]), op=ALU.mult)
            else:
                V("tensor_copy", [tb], [otok], out=o3, in_=t3[:, :, 0:64])
            gain3 = gO[:, hidx * 64:(hidx + 1) * 64].unsqueeze(1).to_broadcast([128, ntile, 64])
            headnorm(o3, otok, ntile, gain3, [(o_n[:, tile0:tile0 + ntile, hidx * 64:(hidx + 1) * 64], o_n)])

        def merge_tile(o_n, ti, x_ap, rows, xm_row0, wout):
            tb = nxt(Tk, "T")
            tbv = tb[:].bitcast(BF16)
            for c in range(8):
                TR(tbv[:, c * 128:(c + 1) * 128], o_n[:, ti, c * 128:(c + 1) * 128], ident_b[:], [o_n, ident_b], [tb])
            h = rr(hT, "h")
            V("tensor_copy", [tb], [h], out=h[:].rearrange("p c t -> p (c t)"), in_=tbv[:, 0:1024])
            x = rr(xt, "x")
            if rows < 128:
                G("memset", [], [x], x[:], 0.0)
            DMA(x[0:rows, :], x_ap, [], [x])
            for half in range(2):
                bank = nxt(A, "A")
                proj(h, wout, half * 512, 512, bank)
                V("tensor_tensor", [bank, x], [x], out=x[:, half * 512:(half + 1) * 512], in0=bank[:, 0:512],
                  in1=x[:, half * 512:(half + 1) * 512], op=ALU.add)
            DMA(xmid[xm_row0:xm_row0 + 128, :], x[:], [x], [])

        stop = getattr(cfg, 'stop', 9)
        o_ns = sbt("o_ns", [128, 1, D], BF16)
        OTsamp = sbt("OTsamp", [65, 6, 128], F32)
        G("memset", [], [OTsamp], OTsamp[:], 1.0)

        def finish_sample(Ob, nh, has_den, h0):
            V("tensor_copy", [Ob], [OTsamp], out=OTsamp[0:65, 0:nh, 0:NTS],
              in_=Ob[0:65, 0:nh * NTS].rearrange("p (h c) -> p h c", h=nh))
            for h in range(nh):
                tb = nxt(Tk, "T")
                TR(tb[:, 0:65], OTsamp[0:65, h, :], ident_f[0:65, 0:65], [OTsamp, ident_f], [tb])
                o3 = otok[:, 0:1, :]
                if has_den:
                    V("reciprocal", [tb], [hs], out=hs[:, 40:41], in_=tb[:, 64:65])
                    V("tensor_scalar", [tb, hs], [otok], out=otok[:, 0, :], in0=tb[:, 0:64], scalar1=hs[:, 40:41],
                      scalar2=None, op0=ALU.mult)
                else:
                    V("tensor_copy", [tb], [otok], out=otok[:, 0, :], in_=tb[:, 0:64])
                hi = h0 + h
                gain3 = gO[:, hi * 64:(hi + 1) * 64].unsqueeze(1).to_broadcast([128, 1, 64])
                headnorm(o3, otok, 1, gain3, [(o_ns[:, 0:1, hi * 64:(hi + 1) * 64], o_ns)])

        def body():
            if stop <= 0:
                return
            with ExitStack() as st:
                NPAIR = NPG // 2
                ptb = sbt("ptb", [128, NBL * NPG], I32, st)
                idxV = sbt("idxV", [128, NBL * NPG], I32, st); idxK = sbt("idxK", [128, NBL * NPAIR], I32, st)
                iot = sbt("iot", [128, 2], F32, st)
                DMA(ptb[:], ptd.rearrange("o n -> (o n)").partition_broadcast(128), [], [ptb])
                G("iota", [], [iot], iot[:, 0:1], pattern=[[0, 1]], base=0, channel_multiplier=1, allow_small_or_imprecise_dtypes=True)
                G("iota", [], [iot], iot[:, 1:2], pattern=[[0, 1]], base=0, channel_multiplier=1, allow_small_or_imprecise_dtypes=True)
                V("tensor_scalar", [iot], [iot], out=iot[64:128, 1:2], in0=iot[64:128, 1:2], scalar1=-64.0, scalar2=None, op0=ALU.add)
                V("tensor_scalar", [ptb, iot], [idxV], out=idxV[:], in0=ptb[:], scalar1=128.0, scalar2=iot[:, 0:1],
                  op0=ALU.mult, op1=ALU.add)
                ptv = ptb[:].rearrange("p (m two) -> p m two", two=2)
                V("tensor_scalar", [ptb, iot], [idxK], out=idxK[0:64, :], in0=ptv[0:64, :, 0], scalar1=64.0,
                  scalar2=iot[0:64, 1:2], op0=ALU.mult, op1=ALU.add)
                V("tensor_scalar", [ptb, iot], [idxK], out=idxK[64:128, :], in0=ptv[64:128, :, 1], scalar1=64.0,
                  scalar2=iot[64:128, 1:2], op0=ALU.mult, op1=ALU.add)

                qdup = sbt("qdup", [128, 6, 128], BF16, st)
                qTf = sbt("qTf", [128, 6, 128], BF16, st); qTs = sbt("qTs", [128, 6, 128], BF16, st)
                kTnf = sbt("kTnf", [64, 6, 128], BF16, st); kTns = sbt("kTns", [64, 6, 128], BF16, st)
                vnf = sbt("vnf", [128, 384], BF16, st); vns = sbt("vns", [128, 384], BF16, st)
                mqTs = sbt("mqTs", [64, 4, 128], BF16, st)
                lfn = sbt("lfn", [128, 6], F32, st); dnew = sbt("dnew", [128, 6], F32, st)
                Cb = sbt("Cb", [128, NBL * 6], F32, st); DT = sbt("DT", [128, NBL * 6], F32, st)

                if stop <= 0.4:
                    return
                hS, xS = norm_T(xs, NTS, gA)
                if stop <= 0.6:
                    return
                hSk = sbt("hSk", [128, 8, 128], BF16, st)
                V("tensor_copy", [hS], [hSk], out=hSk[:], in_=hS[:])

                def tr_heads(src_fn, srcT, nh, rows_out, dst3, dstT):
                    tb = nxt(Tk, "T")
                    tbv = tb[:].bitcast(BF16)
                    for h in range(nh):
                        TR(tbv[0:rows_out, h * 128:(h + 1) * 128], src_fn(h), ident_b[:], [srcT, ident_b], [tb])
                    V("tensor_copy", [tb], [dstT], out=dst3,
                      in_=tbv[0:rows_out, 0:nh * 128].rearrange("p (h t) -> p h t", h=nh))

                load_w(w_in, 0, 0, 1158)
                bank = nxt(A, "A"); proj(hSk, wbuf, 0, 384, bank)
                headnorm(bank[:, 0:384].rearrange("p (h d) -> p h d", h=6), bank, 6, g3(gfq8, 6),
                         [(qdup[:, :, 0:64], qdup), (qdup[:, :, 64:128], qdup)])
                if stop <= 0.7:
                    return
                tr_heads(lambda h: qdup[:, h, :], qdup, 6, 128, qTf[:], qTf)
                if stop <= 0.8:
                    return
                bank = nxt(A, "A"); proj(hSk, wbuf, 384, 384, bank)
                pf = rr(ptmp, "p"); pb_ = rr(pbf, "pb")
                headnorm(bank[:, 0:384].rearrange("p (h d) -> p h d", h=6), bank, 6, g3(gfk, 6),
                         [(pf[:, 0:384].rearrange("p (h d) -> p h d", h=6), pf),
                          (pb_[:, 0:384].rearrange("p (h d) -> p h d", h=6), pb_)])
                DMA(s_fk, pf[0:NTS, 0:384], [pf], [])
                if stop <= 0.82:
                    return
                tr_heads(lambda h: pb_[:, h * 64:(h + 1) * 64], pb_, 6, 64, kTnf[:], kTnf)
                if stop <= 0.85:
                    return
                bank = nxt(A, "A"); proj(hSk, wbuf, 768, 390, bank)
                if stop <= 0.855:
                    return
                pf = rr(ptmp, "p")
                V("tensor_copy", [bank], [pf], out=pf[:, 0:384], in_=bank[:, 0:384])
                if stop <= 0.86:
                    return
                ACT(vnf[:], bank[:, 0:384], AF.Copy, [bank], [vnf])
                if stop <= 0.865:
                    return
                DMA(s_fv, pf[0:NTS, 0:384], [pf], [])
                if stop <= 0.87:
                    return
                logsig(bank[:, 384:390], bank)
                if stop <= 0.88:
                    return
                DMA(s_lf, lfo[0:NTS, :], [lfo], [])
                V("tensor_copy", [lfo], [lfn], out=lfn[:], in_=lfo[:])
                if stop <= 0.9:
                    return
                tb = nxt(Tk, "T")
                MM(tb[:, 0:6], Mrev[:], lfn[:], True, True, [Mrev, lfn], [tb])
                for b in range(NBL):
                    MM(tb[:, 8 + b * 6:14 + b * 6], selB[:, b, :], lfn[:], True, True, [selB, lfn], [tb])
                V("tensor_copy", [tb], [dnew], out=dnew[:], in_=tb[:, 0:6])
                V("tensor_copy", [tb], [Cb], out=Cb[:], in_=tb[:, 8:8 + NBL * 6])
                if stop <= 0.95:
                    return
                load_w(w_in, 0, C_SQ, 1152)
                bank = nxt(A, "A"); proj(hSk, wbuf, 0, 384, bank)
                for half in range(2):
                    V("tensor_scalar", [bank], [qdup], out=qdup[:, :, half * 64:(half + 1) * 64],
                      in0=bank[:, 0:384].rearrange("p (h d) -> p h d", h=6), scalar1=0.125, scalar2=None, op0=ALU.mult)
                tr_heads(lambda h: qdup[:, h, :], qdup, 6, 128, qTs[:], qTs)
                bank = nxt(A, "A"); proj(hSk, wbuf, 384, 384, bank)
                pf = rr(ptmp, "p"); pb_ = rr(pbf, "pb")
                V("tensor_copy", [bank], [pf], out=pf[:, 0:384], in_=bank[:, 0:384])
                ACT(pb_[:, 0:384], bank[:, 0:384], AF.Copy, [bank], [pb_])
                DMA(s_sk, pf[0:NTS, 0:384], [pf], [])
                tr_heads(lambda h: pb_[:, h * 64:(h + 1) * 64], pb_, 6, 64, kTns[:], kTns)
                bank = nxt(A, "A"); proj(hSk, wbuf, 768, 384, bank)
                pf = rr(ptmp, "p")
                V("tensor_copy", [bank], [pf], out=pf[:, 0:384], in_=bank[:, 0:384])
                ACT(vns[:], bank[:, 0:384], AF.Copy, [bank], [vns])
                DMA(s_sv, pf[0:NTS, 0:384], [pf], [])
                load_w(w_in, 0, C_MQ, 256)
                bank = nxt(A, "A"); proj(hSk, wbuf, 0, 256, bank)
                pb_ = rr(pbf, "pb")
                headnorm(bank[:, 0:256].rearrange("p (h d) -> p h d", h=4), bank, 4, g3(gmq8, 4),
                         [(pb_[:, 0:256].rearrange("p (h d) -> p h d", h=4), pb_)])
                tr_heads(lambda h: pb_[:, h * 64:(h + 1) * 64], pb_, 4, 64, mqTs[:], mqTs)

                if stop <= 1:
                    return
                KTb = [sbt("KTb%d" % i, [128, NBL, 768], BF16, st) for i in range(2)]
                Vb = [sbt("Vb%d" % i, [128, 2, NBL, 384], BF16, st) for i in range(2)]
                LF = [sbt("LF%d" % i, [128, 2, NBL, 6], F32, st) for i in range(2)]

                def idma(out, src2d, idx_ap, r, w, group):
                    return sc.dma("pool", lambda e: e.indirect_dma_start(
                        out=out, out_offset=None, in_=src2d, in_offset=bass.IndirectOffsetOnAxis(ap=idx_ap, axis=0)),
                        bl(r), bl(w), group=group)

                for slot in range(2):
                    fox = slot == 0
                    kT2d, v2d = (fkT, fv) if fox else (skT, sv)
                    qT_, kTn_, vn_, mask = (qTf, kTnf, vnf, mnf) if fox else (qTs, kTns, vns, mns)
                    Ob = O[slot]
                    nblk = 1 + NPG
                    bi = 0

                    def pv(pt_, lhs_fn, first, last):
                        n = 0
                        for b in range(NBL):
                            for h in range(6):
                                c0 = h * NTS + b * T
                                lhs, lT = lhs_fn(b, h)
                                MM(Ob[0:64, c0:c0 + T], lhs, pt_[:, c0:c0 + T], first and n == 0, False, [lT, pt_], [Ob])
                                n += 1
                        MM(Ob[64:65, 0:NCS], onecol[:, 0:1], pt_[:, 0:NCS], first, last, [onecol, pt_], [Ob])

                    def zf_new(bank, last=True):
                        for h in range(6):
                            MM(bank[:, h * NTS:(h + 1) * NTS], kTn_[0:64, h, :], qT_[0:64, h, 0:NTS], h == 0, False,
                               [kTn_, qT_], [bank])
                        MM(bank[:, 0:NCS], ident_b[:], mask[:], False, last, [ident_b, mask], [bank])
                    if fox:
                        pt_ = fox_block(zf_new, NCS,
                                        bias3=(lambda t: t[:, 0:NCS].rearrange("p (h c) -> p h c", h=6),
                                               dnew[:, :].unsqueeze(2).to_broadcast([128, 6, NTS])), biasT=dnew)
                    else:
                        pt_ = sb_block(zf_new, NCS, True)
                    pv(pt_, lambda b, h: (vn_[:, h * 64:(h + 1) * 64], vn_), True, False)
                    bi += 1
                    for a in reversed(range(NPAIR)):
                        buf = a % 2
                        grp = []
                        for b in range(NBL):
                            idma(KTb[buf][:, b, :], kT2d, idxK[:, b * NPAIR + a:b * NPAIR + a + 1], [idxK], [KTb[buf]], grp)
                        for pos in range(2):
                            for b in range(NBL):
                                ix = idxV[:, b * NPG + 2 * a + pos:b * NPG + 2 * a + pos + 1]
                                idma(Vb[buf][:, pos, b, :], v2d, ix, [idxV], [Vb[buf]], grp)
                                if fox:
                                    idma(LF[buf][:, pos, b, :], flf, ix, [idxV], [LF[buf]], grp)
                        for pos in (1, 0):
                            bi += 1
                            if fox:
                                tb = nxt(Tk, "T")
                                lf2 = LF[buf][:, pos, :, :].rearrange("p b h -> p (b h)")
                                MM(tb[:, 0:NBL * 6], triU[:], lf2, True, True, [triU, LF[buf]], [tb])
                                MM(tb[:, 32:32 + NBL * 6], ones_f[:], lf2, True, True, [ones_f, LF[buf]], [tb])
                                V("tensor_tensor", [tb, Cb], [DT], out=DT[:], in0=tb[:, 0:NBL * 6], in1=Cb[:], op=ALU.add)
                                V("tensor_tensor", [tb, Cb], [Cb], out=Cb[:], in0=tb[:, 32:32 + NBL * 6], in1=Cb[:], op=ALU.add)

                            def zf_pg(bank, last=True, pos=pos, buf=buf):
                                n = 0
                                for b in range(NBL):
                                    for h in range(6):
                                        c0 = h * NTS + b * T
                                        MM(bank[:, c0:c0 + T], KTb[buf][pos * 64:(pos + 1) * 64, b, h * 128:(h + 1) * 128],
                                           qT_[pos * 64:(pos + 1) * 64, h, b * T:(b + 1) * T], n == 0,
                                           last and n == NBL * 6 - 1, [KTb[buf], qT_], [bank])
                                        n += 1
                            if fox:
                                pt_ = fox_block(zf_pg, NCS, bias3=(
                                    lambda t: t[:, 0:NCS].rearrange("p (h b q) -> p h b q", h=6, b=NBL),
                                    DT[:, :].rearrange("p (b h) -> p h b", b=NBL).unsqueeze(3).to_broadcast([128, 6, NBL, T])),
                                    biasT=DT)
                            else:
                                pt_ = sb_block(zf_pg, NCS, False)
                            pv(pt_, lambda b, h, pos=pos, buf=buf: (Vb[buf][:, pos, b, h * 64:(h + 1) * 64], Vb[buf]),
                               False, bi == nblk)
                    finish_sample(Ob, 6, fox, 0 if fox else 6)

                if stop <= 2:
                    return
                cmk = sbt("cmk", [64, NBL, 4, 256], BF16, st); cmvb = sbt("cmvb", [128, NBL, 2, 256], BF16, st)
                for b in range(NBL):
                    stg = wst[b % 2]
                    DMA(stg[0:64, 0:1024], cmkT[b].rearrange("d h k -> d (h k)"), [], [stg])
                    G("tensor_copy", [stg], [cmk], out=cmk[0:64, b, :, :].rearrange("d h k -> d (h k)"), in_=stg[0:64, 0:1024])
                for b in range(NBL):
                    stg = wst[b % 2]
                    DMA(stg[:, 0:512].rearrange("p (kb c) -> p kb c", kb=2), cmv[b].rearrange("(kb p) c -> p kb c", p=128), [], [stg])
                    G("tensor_copy", [stg], [cmvb], out=cmvb[:, b, :, :].rearrange("p kb c -> p (kb c)"), in_=stg[:, 0:512])
                Ob = O[0]
                NM = 4 * NTS
                for kb in range(2):
                    def zf_m(bank, last=True, kb=kb):
                        n = 0
                        for b in range(NBL):
                            for h in range(4):
                                c0 = h * NTS + b * T
                                MM(bank[:, c0:c0 + T], cmk[0:64, b, h, kb * 128:(kb + 1) * 128], mqTs[0:64, h, b * T:(b + 1) * T],
                                   n == 0, last and n == NBL * 4 - 1, [cmk, mqTs], [bank])
                                n += 1
                    pt_ = fox_block(zf_m, NM)
                    n = 0
                    for b in range(NBL):
                        for h in range(4):
                            c0 = h * NTS + b * T
                            MM(Ob[0:64, c0:c0 + T], cmvb[:, b, kb, h * 64:(h + 1) * 64], pt_[:, c0:c0 + T],
                               kb == 0 and n == 0, False, [cmvb, pt_], [Ob])
                            n += 1
                    MM(Ob[64:65, 0:NM], onecol[:, 0:1], pt_[:, 0:NM], kb == 0, kb == 1, [onecol, pt_], [Ob])
                finish_sample(Ob, 4, True, 12)
            sc.barrier()

            if stop <= 3:
                return
            with ExitStack() as st:
                qT = sbt("qT", [128, 6, S], BF16, st); kT = sbt("kT", [128, 6, S], BF16, st)
                vT = sbt("vT", [128, NT, 6, 65], BF16, st); o_n = sbt("o_n", [128, NT, D], BF16, st)
                mkT = sbt("mkT", [64, 4, 256], BF16, st); mvb = sbt("mvb", [128, 2, 4, 65], BF16, st)
                G("memset", [], [vT], vT[:], 1.0)
                G("memset", [], [mvb], mvb[:], 1.0)

                def tr_to(src_fn, srcT, nh, rows, dstT, tt):
                    tb = nxt(Tk, "T")
                    tbv = tb[:].bitcast(BF16)
                    for h in range(nh):
                        TR(tbv[0:rows, h * 128:(h + 1) * 128], src_fn(h), ident_b[:], [srcT, ident_b], [tb])
                    V("tensor_copy", [tb], [dstT], out=dstT[0:rows, 0:nh, tt * 128:(tt + 1) * 128],
                      in_=tbv[0:rows, 0:nh * 128].rearrange("p (h t) -> p h t", h=nh))

                def v3(ap384):
                    return ap384.rearrange("p (h d) -> p h d", h=6)

                for seq in range(NSEQ):
                    r0 = seq * S

                    def xrows(tt):
                        return xp[r0 + tt * 128:r0 + (tt + 1) * 128, :]
                    load_w(w_in, 0, 0, 1158)
                    G("memset", [], [carry], carry[:], 0.0)
                    h, _ = norm_T(xrows(0), 128, gA)
                    for tt in range(NT):
                        rows = slice(r0 + tt * 128, r0 + (tt + 1) * 128)
                        b1 = nxt(A, "A"); proj(h, wbuf, 0, 384, b1)
                        b2 = nxt(A, "A"); proj(h, wbuf, 384, 384, b2)
                        b3 = nxt(Bk, "B"); proj(h, wbuf, 768, 390, b3)
                        if tt + 1 < NT:
                            h, _ = norm_T(xrows(tt + 1), 128, gA)
                        headnorm(v3(b1[:, 0:384]), b1, 6, g3(gfq8, 6), [(qaug[:, :, 0:64], qaug)])
                        pf = rr(ptmp, "p")
                        headnorm(v3(b2[:, 0:384]), b2, 6, g3(gfk, 6), [(v3(pf[:, 0:384]), pf), (kaug[:, :, 0:64], kaug)])
                        DMA(p_fk[rows, :], pf[:, 0:384], [pf], [])
                        pf = rr(ptmp, "p")
                        V("tensor_copy", [b3], [pf], out=pf[:, 0:384], in_=b3[:, 0:384])
                        ACT(vT[:, tt, :, 0:64], v3(b3[:, 0:384]), AF.Copy, [b3], [vT])
                        DMA(p_fv[rows, :], pf[:, 0:384], [pf], [])
                        logsig(b3[:, 384:390], b3)
                        DMA(p_lf[rows, :], lfo[:], [lfo], [])
                        tb = nxt(Tk, "T")
                        MM(tb[:, 0:6], triL[:], lfo[:], True, True, [triL, lfo], [tb])
                        MM(tb[:, 8:14], ones_f[:], lfo[:], True, True, [ones_f, lfo], [tb])
                        V("tensor_tensor", [tb, carry], [cc], out=cc[:], in0=tb[:, 0:6], in1=carry[:], op=ALU.add)
                        V("tensor_tensor", [tb, carry], [carry], out=carry[:], in0=tb[:, 8:14], in1=carry[:], op=ALU.add)
                        V("tensor_copy", [cc], [chi], out=chi[:], in_=cc[:])
                        V("tensor_tensor", [cc, chi], [clo], out=clo[:], in0=cc[:], in1=chi[:], op=ALU.subtract)
                        V("tensor_copy", [chi], [qaug], out=qaug[:, :, 64], in_=chi[:])
                        V("tensor_copy", [clo], [qaug], out=qaug[:, :, 65], in_=clo[:])
                        V("tensor_scalar", [cc], [kaug], out=kaug[:, :, 66], in0=cc[:], scalar1=-1.0, scalar2=None, op0=ALU.mult)
                        V("tensor_scalar", [clo], [kaug], out=kaug[:, :, 67], in0=clo[:], scalar1=-1.0, scalar2=None, op0=ALU.mult)
                        tr_to(lambda hh: qaug[:, hh, :], qaug, 6, 68, qT, tt)
                        tr_to(lambda hh: kaug[:, hh, :], kaug, 6, 68, kT, tt)
                    for r in range(NR):
                        for hh in range(6):
                            Ob = nxt(O, "O")
                            nb = 4 * r + 4

                            def zf_k(i, hh=hh, r=r):
                                kb = i

                                def zf(bank, last=True):
                                    diag = kb >= 4 * r
                                    MM(bank[:, 0:512], kT[0:68, hh, kb * 128:(kb + 1) * 128], qT[0:68, hh, r * 512:(r + 1) * 512],
                                       True, last and not diag, [kT, qT], [bank])
                                    if diag:
                                        MM(bank[:, 0:512], ident_b[:], mfox[:, kb - 4 * r, :], False, last, [ident_b, mfox], [bank])
                                return zf
                            fox_attn(nb, zf_k, lambda i, hh=hh: (vT[:, i, hh, :], vT), Ob)
                            finish_head(Ob[0:65, 0:512], Ob, 4, True, hh, o_n, 4 * r)
                    load_w(w_in, 0, C_SQ, 1152)
                    h, _ = norm_T(xrows(0), 128, gA)
                    for tt in range(NT):
                        rows = slice(r0 + tt * 128, r0 + (tt + 1) * 128)
                        b1 = nxt(A, "A"); proj(h, wbuf, 0, 384, b1)
                        b2 = nxt(A, "A"); proj(h, wbuf, 384, 384, b2)
                        b3 = nxt(Bk, "B"); proj(h, wbuf, 768, 384, b3)
                        if tt + 1 < NT:
                            h, _ = norm_T(xrows(tt + 1), 128, gA)
                        pb_ = rr(pbf, "pb")
                        ACT(pb_[:, 0:384], b1[:, 0:384], AF.Copy, [b1], [pb_], scale=0.125)
                        tr_to(lambda hh: pb_[:, hh * 64:(hh + 1) * 64], pb_, 6, 64, qT, tt)
                        pf = rr(ptmp, "p"); pb_ = rr(pbf, "pb")
                        V("tensor_copy", [b2], [pf], out=pf[:, 0:384], in_=b2[:, 0:384])
                        ACT(pb_[:, 0:384], b2[:, 0:384], AF.Copy, [b2], [pb_])
                        DMA(p_sk[rows, :], pf[:, 0:384], [pf], [])
                        tr_to(lambda hh: pb_[:, hh * 64:(hh + 1) * 64], pb_, 6, 64, kT, tt)
                        pf = rr(ptmp, "p")
                        V("tensor_copy", [b3], [pf], out=pf[:, 0:384], in_=b3[:, 0:384])
                        ACT(vT[:, tt, :, 0:64], v3(b3[:, 0:384]), AF.Copy, [b3], [vT])
                        DMA(p_sv[rows, :], pf[:, 0:384], [pf], [])
                    for r in range(NR):
                        for hh in range(6):
                            Ob = nxt(O, "O")
                            nb = 4 * r + 4

                            def zf_k(i, hh=hh, r=r, nb=nb):
                                kb = nb - 1 - i

                                def zf(bank, last=True):
                                    diag = kb >= 4 * r
                                    MM(bank[:, 0:512], kT[0:64, hh, kb * 128:(kb + 1) * 128], qT[0:64, hh, r * 512:(r + 1) * 512],
                                       True, last and not diag, [kT, qT], [bank])
                                    if diag:
                                        MM(bank[:, 0:512], ident_b[:], msb[:, kb - 4 * r, :], False, last, [ident_b, msb], [bank])
                                return zf
                            sb_attn(nb, zf_k, lambda i, hh=hh, nb=nb: (vT[:, nb - 1 - i, hh, :], vT), Ob)
                            finish_head(Ob[0:65, 0:512], Ob, 4, False, 6 + hh, o_n, 4 * r)
                    load_w(w_mem, 0, 0, 512)
                    DMA(gX[:], bcast_row(g_mem_in, D), [], [gX])
                    for mt in range(2):
                        rows = slice(seq * 256 + mt * 128, seq * 256 + (mt + 1) * 128)
                        h, _ = norm_T(memp[rows, :], 128, gX)
                        bank = nxt(A, "A"); proj(h, wbuf, 0, 512, bank)
                        pf = rr(ptmp, "p"); pb_ = rr(pbf, "pb")
                        headnorm(bank[:, 0:256].rearrange("p (h d) -> p h d", h=4), bank, 4, g3(gmk, 4),
                                 [(pf[:, 0:256].rearrange("p (h d) -> p h d", h=4), pf),
                                  (pb_[:, 0:256].rearrange("p (h d) -> p h d", h=4), pb_)])
                        DMA(p_mk[rows, :], pf[:, 0:256], [pf], [])
                        tb = nxt(Tk, "T")
                        tbv = tb[:].bitcast(BF16)
                        for hh in range(4):
                            TR(tbv[0:64, hh * 128:(hh + 1) * 128], pb_[:, hh * 64:(hh + 1) * 64], ident_b[:], [pb_, ident_b], [tb])
                        V("tensor_copy", [tb], [mkT], out=mkT[0:64, :, mt * 128:(mt + 1) * 128],
                          in_=tbv[0:64, 0:512].rearrange("p (h t) -> p h t", h=4))
                        pf = rr(ptmp, "p")
                        V("tensor_copy", [bank], [pf], out=pf[:, 0:256], in_=bank[:, 256:512])
                        ACT(mvb[:, mt, :, 0:64], bank[:, 256:512].rearrange("p (h d) -> p h d", h=4), AF.Copy, [bank], [mvb])
                        DMA(p_mv[rows, :], pf[:, 0:256], [pf], [])
                    load_w(w_in, 0, C_MQ, 256)
                    h, _ = norm_T(xrows(0), 128, gA)
                    for tt in range(NT):
                        bank = nxt(A, "A"); proj(h, wbuf, 0, 256, bank)
                        if tt + 1 < NT:
                            h, _ = norm_T(xrows(tt + 1), 128, gA)
                        pb_ = rr(pbf, "pb")
                        headnorm(bank[:, 0:256].rearrange("p (h d) -> p h d", h=4), bank, 4, g3(gmq8, 4),
                                 [(pb_[:, 0:256].rearrange("p (h d) -> p h d", h=4), pb_)])
                        tr_to(lambda hh: pb_[:, hh * 64:(hh + 1) * 64], pb_, 4, 64, qT, tt)
                    for r in range(NR):
                        for hh in range(4):
                            Ob = nxt(O, "O")

                            def zf_k(i, hh=hh, r=r):
                                kb = i

                                def zf(bank, last=True):
                                    MM(bank[:, 0:512], mkT[0:64, hh, kb * 128:(kb + 1) * 128], qT[0:64, hh, r * 512:(r + 1) * 512],
                                       True, last, [mkT, qT], [bank])
                                return zf
                            fox_attn(2, zf_k, lambda i, hh=hh: (mvb[:, i, hh, :], mvb), Ob)
                            finish_head(Ob[0:65, 0:512], Ob, 4, True, 12 + hh, o_n, 4 * r)
                    load_w(w_out, 0, 0, 1024)
                    for tt in range(NT):
                        merge_tile(o_n, tt, xrows(tt), 128, r0 + tt * 128, wbuf)
                merge_tile(o_ns, 0, xs, NTS, NSEQ * S, wbuf)
            sc.barrier()
            esA.close()

            if stop <= 4:
                return
            with ExitStack() as st:
                wup = sbt("wup", [128, 8, DFF], BF16, st); wdn = sbt("wdn", [128, NFC, D], BF16, st)
                uT = sbt("uT", [128, NFC, 256], BF16, st)
                xm = [sbt("xm0", [128, D], F32, st), sbt("xm1", [128, D], F32, st)]
                rl = [sbt("rl0", [128, 256], BF16, st), sbt("rl1", [128, 256], BF16, st)]
                DMA(gX[:], bcast_row(g_ffn, D), [], [gX])
                for kc in range(8):
                    for c0 in range(0, DFF, 1024):
                        n = min(1024, DFF - c0)
                        stg = wst[(kc + c0 // 1024) % 2]
                        DMA(stg[:, 0:n], w_up[kc * 128:(kc + 1) * 128, c0:c0 + n], [], [stg])
                        G("tensor_copy", [stg], [wup], out=wup[:, kc, c0:c0 + n], in_=stg[:, 0:n])
                for fc in range(NFC):
                    stg = wst[fc % 2]
                    DMA(stg[:, 0:D], w_down[fc * 128:(fc + 1) * 128, :], [], [stg])
                    G("tensor_copy", [stg], [wdn], out=wdn[:, fc, :], in_=stg[:, 0:D])
                groups = [(g * 256, 2) for g in range(NSEQ * S // 256)] + [(NSEQ * S, 1)]
                for (row0, ntile) in groups:
                    ntok = ntile * 128
                    hs_ = []
                    for i in range(ntile):
                        DMA(xm[i][:], xmid[row0 + i * 128:row0 + (i + 1) * 128, :], [], [xm[i]])
                        h, _ = norm_T(None, 128, gX, src_t=xm[i])
                        hs_.append(h)
                    for fc in range(NFC):
                        bank = nxt(A, "A")
                        for i in range(ntile):
                            for kc in range(8):
                                MM(bank[:, i * 128:(i + 1) * 128], wup[:, kc, fc * 128:(fc + 1) * 128], hs_[i][:, kc, :],
                                   kc == 0, kc == 7, [wup, hs_[i]], [bank])
                        r_ = rl[fc % 2]
                        ACT(r_[:, 0:ntok], bank[:, 0:ntok], AF.Relu, [bank], [r_])
                        V("tensor_tensor", [r_], [uT], out=uT[:, fc, 0:ntok], in0=r_[:, 0:ntok], in1=r_[:, 0:ntok], op=ALU.mult)
                    for i in range(ntile):
                        for half in range(2):
                            bank = nxt(Bk, "B")
                            for fc in range(NFC):
                                MM(bank[:, 0:512], uT[:, fc, i * 128:(i + 1) * 128], wdn[:, fc, half * 512:(half + 1) * 512],
                                   fc == 0, fc == NFC - 1, [uT, wdn], [bank])
                            yt = rr(ptmp, "p")
                            V("tensor_tensor", [bank, xm[i]], [yt], out=yt[:, 0:512], in0=bank[:, 0:512],
                              in1=xm[i][:, half * 512:(half + 1) * 512], op=ALU.add)
                            rw = row0 + i * 128
                            if rw < NSEQ * S:
                                DMA(yp[rw:rw + 128, half * 512:(half + 1) * 512], yt[:, 0:512], [yt], [])
                            else:
                                DMA(ys[0:NTS, half * 512:(half + 1) * 512], yt[0:NTS, 0:512], [yt], [])
        try:
            body()
        except _Stop:
            pass
        esA.close()
        sc.finish("sp")
    return nc


def _prep(inputs, cfg, ncores=8):
    f = lambda a: np.ascontiguousarray(np.asarray(a))
    NSEQ, S, NPG, NBL, T = cfg.NSEQ, cfg.S, cfg.NPG, cfg.NBL, cfg.T
    fk = np.asarray(inputs["cache_fox_k"])[0]; sk = np.asarray(inputs["cache_sb_k"])[0]
    npool = fk.shape[0]
    shared = {
        "fkT": f(fk.transpose(0, 3, 2, 1)).reshape(npool * 64, 768),
        "skT": f(sk.transpose(0, 3, 2, 1)).reshape(npool * 64, 768),
        "fv": f(np.asarray(inputs["cache_fox_v"])[0]).reshape(npool * 128, 384),
        "sv": f(np.asarray(inputs["cache_sb_v"])[0]).reshape(npool * 128, 384),
        "flf": f(np.asarray(inputs["cache_fox_logf"])[0]).reshape(npool * 128, 6),
    }
    for nm, key in (("g_attn", "g_attn"), ("b_f", "b_forget"), ("g_fq", "g_fox_q"), ("g_fk", "g_fox_k"),
                    ("g_mem_in", "g_mem_in"), ("g_mq", "g_mem_q"), ("g_mk", "g_mem_k"), ("g_out", "g_out"),
                    ("g_ffn", "g_ffn")):
        shared[nm] = f(np.asarray(inputs[key])[0]).reshape(1, -1)
    for nm, key in (("w_in", "w_in"), ("w_mem", "w_mem_kv"), ("w_out", "w_out"), ("w_up", "w_up"), ("w_down", "w_down")):
        shared[nm] = f(np.asarray(inputs[key])[0])
    xp = np.asarray(inputs["x_prompt"]); xsm = np.asarray(inputs["x_sample"]); mem = np.asarray(inputs["mem_prompt"])
    cmk = np.asarray(inputs["cache_mem_k"])[0]; cmv = np.asarray(inputs["cache_mem_v"])[0]
    pt = np.asarray(inputs["page_table"]).astype(np.int32)
    maps = []
    for c in range(ncores):
        m = dict(shared)
        m["xp"] = f(xp[c * NSEQ:(c + 1) * NSEQ]).reshape(NSEQ * S, D)
        m["memp"] = f(mem[c * NSEQ:(c + 1) * NSEQ]).reshape(NSEQ * 256, D)
        m["xs"] = f(xsm[c * NBL:(c + 1) * NBL]).reshape(NBL * T, D)
        m["cmkT"] = f(cmk[c * NBL:(c + 1) * NBL].transpose(0, 3, 2, 1))
        m["cmv"] = f(cmv[c * NBL:(c + 1) * NBL]).reshape(NBL, 256, 256)
        m["pt"] = f(pt[c * NBL:(c + 1) * NBL]).reshape(1, NBL * NPG)
        maps.append(m)
    return maps


def _assemble(res, cfg, ncores=8):
    NSEQ, S, NBL, T = cfg.NSEQ, cfg.S, cfg.NBL, cfg.T
    cat = lambda k: np.concatenate([np.asarray(r[k]) for r in res], axis=0)
    B = ncores * NSEQ; Bs = ncores * NBL
    return (cat("yp").reshape(B, S, D), cat("ys").reshape(Bs, T, D),
            cat("p_fk").reshape(1, B, S, 6, 64), cat("p_fv").reshape(1, B, S, 6, 64), cat("p_lf").reshape(1, B, S, 6),
            cat("p_sk").reshape(1, B, S, 6, 64), cat("p_sv").reshape(1, B, S, 6, 64),
            cat("p_mk").reshape(1, B, 256, 4, 64), cat("p_mv").reshape(1, B, 256, 4, 64),
            cat("s_fk").reshape(1, Bs, T, 6, 64), cat("s_fv").reshape(1, Bs, T, 6, 64), cat("s_lf").reshape(1, Bs, T, 6),
            cat("s_sk").reshape(1, Bs, T, 6, 64), cat("s_sv").reshape(1, Bs, T, 6, 64))


def run(inputs, cfg):
    nc = build(cfg)
    maps = _prep(inputs, cfg)
    res = run_bass_kernel_spmd(nc, maps, core_ids=list(range(8)))
    return _assemble(res.results, cfg)


def kernel(**inputs):
    return run(inputs, Cfg())
```

```python
from contextlib import ExitStack
import numpy as np
import concourse.bass as bass
import concourse.mybir as mybir
from concourse.bass_utils import run_bass_kernel_spmd

F32 = mybir.dt.float32
BF16 = mybir.dt.bfloat16
I32 = mybir.dt.int32
AF = mybir.ActivationFunctionType
ALU = mybir.AluOpType
AX = mybir.AxisListType
EPS = 1e-6
NEG = -30000.0
D = 1024
NIN = 2566
C_FQ, C_FK, C_FV, C_FG, C_SQ, C_SK, C_SV, C_MQ = 0, 384, 768, 1152, 1158, 1542, 1926, 2310


class Cfg:
    def __init__(self, NSEQ=2, S=2048, NPG=128, NPOOL=5120, NBL=4, T=8, DFF=4096):
        self.NSEQ, self.S, self.NPG, self.NPOOL, self.NBL, self.T, self.DFF = NSEQ, S, NPG, NPOOL, NBL, T, DFF


class Buf:
    __slots__ = ("name", "w", "r", "excl")

    def __init__(self, name, excl=False):
        self.name = name
        self.w = None
        self.r = {}
        self.excl = excl


class Sched:
    EPOCH = 30000

    def __init__(self, nc, es, n_dma_sems=32):
        self.nc = nc
        self.es = es
        self.engs = {"pe": nc.tensor, "act": nc.scalar, "dve": nc.vector, "pool": nc.gpsimd, "sp": nc.sync}
        self.cur = {}
        self.cnt = {}
        self.nsem = 0
        self.waited = {e: {} for e in self.engs}
        for e in self.engs:
            self._new_epoch(e)
        self.dma_sems = [[self._sem("d%d" % i), 0] for i in range(n_dma_sems)]
        self.dma_rr = 0
        self.last_ev = {e: None for e in self.engs}
        self.all_dma_evs = []

    def _sem(self, name):
        self.nsem += 1
        s = self.es.enter_context(self.nc.semaphore("s%d_%s" % (self.nsem, name)))
        return (s, self.nsem)

    def _new_epoch(self, e):
        self.cur[e] = self._sem(e)
        self.cnt[e] = 0

    def wait(self, e, ev):
        if ev is None:
            return
        sem, val, key = ev
        if self.waited[e].get(key, 0) >= val:
            return
        if e in ("pe", "sp") and key == self.cur[e][1]:
            return
        self.engs[e].wait_ge(sem, val)
        self.waited[e][key] = val

    def _deps(self, e, reads, writes):
        for b in reads:
            self.wait(e, b.w)
        for b in writes:
            self.wait(e, b.w)
            for ev in b.r.values():
                self.wait(e, ev)

    def _commit(self, ev, reads, writes):
        for b in reads:
            b.r[ev[2]] = ev
        for b in writes:
            b.w = ev
            b.r = {}

    def op(self, e, fn, reads=(), writes=()):
        if self.cnt[e] >= self.EPOCH:
            self._new_epoch(e)
        writes = list(writes) + [b for b in reads if b.excl and b not in writes]
        reads = [b for b in reads if not b.excl]
        self._deps(e, reads, writes)
        inst = fn(self.engs[e])
        sem, key = self.cur[e]
        self.cnt[e] += 1
        inst.then_inc(sem, 1)
        ev = (sem, self.cnt[e], key)
        self.last_ev[e] = ev
        self._commit(ev, reads, writes)
        return ev

    def dma(self, q, fn, reads=(), writes=(), group=None):
        self._deps(q, reads, writes)
        if group is not None and group:
            slot = group[0]
        else:
            slot = self.dma_sems[self.dma_rr % len(self.dma_sems)]
            self.dma_rr += 1
            (sem, key), val = slot
            self.wait(q, (sem, val, key))
            if group is not None:
                group.append(slot)
        (sem, key), val = slot
        inst = fn(self.engs[q])
        inst.then_inc(sem, 16)
        slot[1] = val + 16
        ev = (sem, val + 16, key)
        self._commit(ev, reads, writes)
        self.all_dma_evs.append(ev)
        if len(self.all_dma_evs) > 256:
            self._compact()
        return ev

    def _compact(self):
        devs = {}
        for ev in self.all_dma_evs:
            if ev[2] not in devs or devs[ev[2]][1] < ev[1]:
                devs[ev[2]] = ev
        self.all_dma_evs = list(devs.values())

    def barrier(self):
        self._compact()
        evs = [ev for ev in self.last_ev.values() if ev is not None]
        for e in self.engs:
            for ev in evs:
                self.wait(e, ev)
            for ev in self.all_dma_evs:
                self.wait(e, ev)

    def finish(self, e="sp"):
        self._compact()
        for ev in self.all_dma_evs:
            self.wait(e, ev)
        for ev in self.last_ev.values():
            self.wait(e, ev)


class _Stop(Exception):
    pass


class T_:
    def __init__(self, t, name):
        self.t = t
        self.b = Buf(name)

    def __getitem__(self, k):
        return self.t[k]


def build(cfg):
    NSEQ, S, NPG, NPOOL, NBL, T, DFF = cfg.NSEQ, cfg.S, cfg.NPG, cfg.NPOOL, cfg.NBL, cfg.T, cfg.DFF
    NT = S // 128
    NR = S // 512
    NTS = NBL * T
    NCS = 6 * NTS
    NFC = DFF // 128
    nc = bass.Bass("TRN2", target_bir_lowering=False)

    def din(name, shape, dt=F32):
        return nc.dram_tensor(name, list(shape), dt, kind="ExternalInput").ap()

    def dout(name, shape, dt=F32):
        return nc.dram_tensor(name, list(shape), dt, kind="ExternalOutput").ap()

    xp = din("xp", [NSEQ * S, D]); xs = din("xs", [NTS, D]); memp = din("memp", [NSEQ * 256, D])
    fkT = din("fkT", [NPOOL * 64, 768]); fv = din("fv", [NPOOL * 128, 384]); flf = din("flf", [NPOOL * 128, 6])
    skT = din("skT", [NPOOL * 64, 768]); sv = din("sv", [NPOOL * 128, 384])
    cmkT = din("cmkT", [NBL, 64, 4, 256]); cmv = din("cmv", [NBL, 256, 256])
    ptd = din("pt", [1, NBL * NPG], I32)
    g_attn = din("g_attn", [1, D]); w_in = din("w_in", [D, NIN]); b_f = din("b_f", [1, 6])
    g_fq = din("g_fq", [1, 64]); g_fk = din("g_fk", [1, 64]); g_mem_in = din("g_mem_in", [1, D])
    w_mem = din("w_mem", [D, 512]); g_mq = din("g_mq", [1, 64]); g_mk = din("g_mk", [1, 64])
    g_out = din("g_out", [1, D]); w_out = din("w_out", [D, D]); g_ffn = din("g_ffn", [1, D])
    w_up = din("w_up", [D, DFF]); w_down = din("w_down", [DFF, D])
    yp = dout("yp", [NSEQ * S, D]); ys = dout("ys", [NTS, D])
    p_fk = dout("p_fk", [NSEQ * S, 384]); p_fv = dout("p_fv", [NSEQ * S, 384]); p_lf = dout("p_lf", [NSEQ * S, 6])
    p_sk = dout("p_sk", [NSEQ * S, 384]); p_sv = dout("p_sv", [NSEQ * S, 384])
    p_mk = dout("p_mk", [NSEQ * 256, 256]); p_mv = dout("p_mv", [NSEQ * 256, 256])
    s_fk = dout("s_fk", [NTS, 384]); s_fv = dout("s_fv", [NTS, 384]); s_lf = dout("s_lf", [NTS, 6])
    s_sk = dout("s_sk", [NTS, 384]); s_sv = dout("s_sv", [NTS, 384])
    xmid = nc.dram_tensor("xmid", [NSEQ * S + 128, D], F32, kind="Internal").ap()

    with ExitStack() as es:
        sc = Sched(nc, es)

        esA = ExitStack()
        dflt = [es]

        def sbt(name, shape, dt, st=None):
            return T_((st or dflt[0]).enter_context(nc.sbuf_tensor(name, list(shape), dt)), name)

        def pst(name):
            t = T_(es.enter_context(nc.psum_tensor(name, [128, 512], F32)), name)
            t.b.excl = True
            return t

        def bl(ts):
            return [t.b for t in ts]

        def ACT(out, in_, func, r, w, **kw):
            return sc.op("act", lambda e: e.activation(out=out, in_=in_, func=func, **kw), bl(r), bl(w))

        def MM(out, lhsT, rhs, start, stop, r, w):
            return sc.op("pe", lambda e: e.matmul(out, lhsT=lhsT, rhs=rhs, start=start, stop=stop), bl(r), bl(w))

        def TR(out, in_, ident, r, w):
            return sc.op("pe", lambda e: e.transpose(out=out, in_=in_, identity=ident), bl(r), bl(w))

        def V(fn, r, w, *a, **kw):
            return sc.op("dve", lambda e: getattr(e, fn)(*a, **kw), bl(r), bl(w))

        def G(fn, r, w, *a, **kw):
            return sc.op("pool", lambda e: getattr(e, fn)(*a, **kw), bl(r), bl(w))

        def DMA(out, in_, r, w, q="sp", **kw):
            return sc.dma(q, lambda e: e.dma_start(out=out, in_=in_, **kw), bl(r), bl(w))

        def bcast_row(ap_row, n):
            return ap_row.rearrange("o n -> (o n)").partition_broadcast(128)

        A = [pst("A0"), pst("A1")]
        Bk = [pst("B0"), pst("B1")]
        O = [pst("O0"), pst("O1")]
        Tk = [pst("T0"), pst("T1")]
        ctr = {"A": 0, "B": 0, "O": 0, "T": 0}

        def nxt(lst, k):
            ctr[k] += 1
            return lst[ctr[k] % len(lst)]

        ident_f = sbt("ident_f", [128, 128], F32); ident_b = sbt("ident_b", [128, 128], BF16)
        gX = sbt("gX", [128, D], F32)
        junk = sbt("junk", [128, D], BF16); hb = sbt("hb", [128, D], BF16)
        ss = sbt("ss", [128, 4], F32)
        hT = [sbt("hT0", [128, 8, 128], BF16), sbt("hT1", [128, 8, 128], BF16)]
        ptmp = [sbt("ptmp0", [128, 512], F32), sbt("ptmp1", [128, 512], F32)]
        wst = [sbt("wst0", [128, 1280], F32), sbt("wst1", [128, 1280], F32)]
        dflt[0] = esA
        triL = sbt("triL", [128, 128], F32); ones_f = sbt("ones_f", [128, 128], F32)
        triU = sbt("triU", [128, 128], F32)
        triSB = sbt("triSB", [128, 128], BF16); negones = sbt("negones", [128, 128], BF16)
        onecol = sbt("onecol", [128, 1], BF16)
        G("memset", [], [ident_f], ident_f[:], 1.0)
        G("affine_select", [ident_f], [ident_f], out=ident_f[:], in_=ident_f[:], pattern=[[-1, 128]],
          compare_op=ALU.is_equal, fill=0.0, base=0, channel_multiplier=1)
        G("tensor_copy", [ident_f], [ident_b], out=ident_b[:], in_=ident_f[:])
        G("memset", [], [ones_f], ones_f[:], 1.0)
        G("memset", [], [triL], triL[:], 1.0)
        G("affine_select", [triL], [triL], out=triL[:], in_=triL[:], pattern=[[1, 128]],
          compare_op=ALU.is_ge, fill=0.0, base=0, channel_multiplier=-1)
        G("memset", [], [triU], triU[:], 1.0)
        G("affine_select", [triU], [triU], out=triU[:], in_=triU[:], pattern=[[-1, 128]],
          compare_op=ALU.is_gt, fill=0.0, base=0, channel_multiplier=1)
        G("memset", [], [triSB], triSB[:], -1.0)
        G("affine_select", [triSB], [triSB], out=triSB[:], in_=triSB[:], pattern=[[-1, 128]],
          compare_op=ALU.is_ge, fill=0.0, base=0, channel_multiplier=1)
        G("memset", [], [negones], negones[:], -1.0)
        G("memset", [], [onecol], onecol[:], 1.0)
        mfox = sbt("mfox", [128, 4, 512], BF16); msb = sbt("msb", [128, 4, 512], BF16)
        for i in range(4):
            for (m, op) in ((mfox, ALU.is_ge), (msb, ALU.is_gt)):
                G("memset", [], [m], m[:, i, :], 0.0)
                G("affine_select", [m], [m], out=m[:, i, :], in_=m[:, i, :], pattern=[[1, 512]],
                  compare_op=op, fill=NEG, base=-128 * i, channel_multiplier=-1)
        mnf = sbt("mnf", [128, NCS], BF16); mns = sbt("mns", [128, NCS], BF16)
        for (m, op) in ((mnf, ALU.is_ge), (mns, ALU.is_gt)):
            G("memset", [], [m], m[:], 0.0)
            G("affine_select", [m], [m], out=m[:], in_=m[:], pattern=[[T, NBL], [0, 6], [1, T]],
              compare_op=op, fill=NEG, base=0, channel_multiplier=-1)
            G("affine_select", [m], [m], out=m[:], in_=m[:], pattern=[[-T, NBL], [0, 6], [0, T]],
              compare_op=ALU.is_ge, fill=NEG, base=0, channel_multiplier=1)
        Mrev = sbt("Mrev", [128, 128], F32)
        G("memset", [], [Mrev], Mrev[:], 1.0)
        G("affine_select", [Mrev], [Mrev], out=Mrev[:], in_=Mrev[:], pattern=[[-1, 128]],
          compare_op=ALU.is_gt, fill=0.0, base=0, channel_multiplier=1)
        G("affine_select", [Mrev], [Mrev], out=Mrev[:], in_=Mrev[:], pattern=[[T, 128 // T], [0, T]],
          compare_op=ALU.is_gt, fill=0.0, base=T, channel_multiplier=-1)
        selB = sbt("selB", [128, NBL, 128], F32)
        G("memset", [], [selB], selB[:], 1.0)
        for b in range(NBL):
            G("affine_select", [selB], [selB], out=selB[:, b, :], in_=selB[:, b, :], pattern=[[0, 128]],
              compare_op=ALU.is_ge, fill=0.0, base=-T * b, channel_multiplier=1)
            G("affine_select", [selB], [selB], out=selB[:, b, :], in_=selB[:, b, :], pattern=[[0, 128]],
              compare_op=ALU.is_ge, fill=0.0, base=T * b + T - 1, channel_multiplier=-1)
        gA = sbt("gA", [128, D], F32); gO = sbt("gO", [128, D], F32)
        DMA(gA[:], bcast_row(g_attn, D), [], [gA]); DMA(gO[:], bcast_row(g_out, D), [], [gO])
        gfq8 = sbt("gfq8", [128, 64], F32); gfk = sbt("gfk", [128, 64], F32)
        gmq8 = sbt("gmq8", [128, 64], F32); gmk = sbt("gmk", [128, 64], F32); bfb = sbt("bfb", [128, 6], F32)
        DMA(gfq8[:], bcast_row(g_fq, 64), [], [gfq8]); DMA(gfk[:], bcast_row(g_fk, 64), [], [gfk])
        DMA(gmq8[:], bcast_row(g_mq, 64), [], [gmq8]); DMA(gmk[:], bcast_row(g_mk, 64), [], [gmk])
        DMA(bfb[:], bcast_row(b_f, 6), [], [bfb])
        V("tensor_scalar", [gfq8], [gfq8], out=gfq8[:], in0=gfq8[:], scalar1=0.125, scalar2=None, op0=ALU.mult)
        V("tensor_scalar", [gmq8], [gmq8], out=gmq8[:], in0=gmq8[:], scalar1=0.125, scalar2=None, op0=ALU.mult)

        xt = [sbt("xt0", [128, D], F32), sbt("xt1", [128, D], F32)]
        hsq = sbt("hsq", [128, 512], F32); hs = sbt("hs", [128, 48], F32); hn2 = sbt("hn2", [128, 512], F32)
        pbf = [sbt("pbf0", [128, 512], BF16), sbt("pbf1", [128, 512], BF16)]
        PT = [sbt("PT0", [128, 512], BF16), sbt("PT1", [128, 512], BF16)]
        spb = [sbt("spb0", [128, 512], BF16), sbt("spb1", [128, 512], BF16)]
        etmp = sbt("etmp", [128, 512], F32)
        etmps = [etmp, sbt("etmp1", [128, 512], F32)]
        raccf = sbt("raccf", [128, 512], F32); raccb = sbt("raccb", [128, 512], BF16)
        OTs = sbt("OTs", [128, 512], F32); otok = sbt("otok", [128, 4, 64], F32)
        lft = sbt("lft", [128, 24], F32); lfo = sbt("lfo", [128, 6], F32)
        carry = sbt("carry", [128, 6], F32); cc = sbt("cc", [128, 6], F32); chi = sbt("chi", [128, 6], BF16)
        clo = sbt("clo", [128, 6], F32)
        qaug = sbt("qaug", [128, 6, 68], BF16); kaug = sbt("kaug", [128, 6, 68], BF16)
        wbuf = sbt("wbuf", [128, 8, 1158], BF16)
        k = {"x": 0, "h": 0, "p": 0, "pb": 0, "PT": 0, "sp": 0, "e": 0}

        def rr(lst, key):
            k[key] += 1
            return lst[k[key] % len(lst)]

        G("memset", [], [qaug], qaug[:], 1.0)
        G("memset", [], [kaug], kaug[:], 1.0)

        def load_w(dram, r0_unused, c0, n, dst=None):
            dst = dst or wbuf
            for kc in range(8):
                stg = wst[kc % 2]
                DMA(stg[:, 0:n], dram[kc * 128:(kc + 1) * 128, c0:c0 + n], [], [stg])
                G("tensor_copy", [stg], [dst], out=dst[:, kc, 0:n], in_=stg[:, 0:n])

        def norm_T(x_ap, rows, gain, src_t=None):
            if src_t is None:
                x = rr(xt, "x")
                if rows < 128:
                    G("memset", [], [x], x[:], 0.0)
                DMA(x[0:rows, :], x_ap, [], [x])
            else:
                x = src_t
            ACT(junk[:], x[:], AF.Square, [x], [junk, ss], accum_out=ss[:, 0:1])
            ACT(ss[:, 1:2], ss[:, 0:1], AF.Ln, [ss], [ss], scale=1.0 / D, bias=EPS)
            ACT(ss[:, 2:3], ss[:, 1:2], AF.Exp, [ss], [ss], scale=-0.5)
            V("scalar_tensor_tensor", [x, ss, gain], [hb], out=hb[:], in0=x[:], scalar=ss[:, 2:3], in1=gain[:],
              op0=ALU.mult, op1=ALU.mult)
            tb = nxt(Tk, "T")
            tbv = tb[:].bitcast(BF16)
            for c in range(8):
                TR(tbv[:, c * 128:(c + 1) * 128], hb[:, c * 128:(c + 1) * 128], ident_b[:], [hb, ident_b], [tb])
            h = rr(hT, "h")
            V("tensor_copy", [tb], [h], out=h[:].rearrange("p c t -> p (c t)"), in_=tbv[:, 0:1024])
            return h, x

        def proj(h, w, c0, n, bank):
            for kc in range(8):
                MM(bank[:, 0:n], h[:, kc, :], w[:, kc, c0:c0 + n], kc == 0, kc == 7, [h, w], [bank])

        def headnorm(src3, srcT, H, gain3, outs):
            n = H * 64
            sq3 = hsq[:, 0:n].rearrange("p (h d) -> p h d", h=H)
            ACT(sq3, src3, AF.Square, [srcT], [hsq])
            V("reduce_sum", [hsq], [hs], out=hs[:, 0:H], in_=sq3, axis=AX.X)
            ACT(hs[:, 16:16 + H], hs[:, 0:H], AF.Ln, [hs], [hs], scale=1.0 / 64, bias=EPS)
            ACT(hs[:, 32:32 + H], hs[:, 16:16 + H], AF.Exp, [hs], [hs], scale=-0.5)
            t3 = hn2[:, 0:n].rearrange("p (h d) -> p h d", h=H)
            V("tensor_tensor", [srcT, hs], [hn2], out=t3, in0=src3,
              in1=hs[:, 32:32 + H].unsqueeze(2).to_broadcast([128, H, 64]), op=ALU.mult)
            for (o3, oT) in outs:
                V("tensor_tensor", [hn2], [oT], out=o3, in0=t3, in1=gain3, op=ALU.mult)

        def g3(gt, H):
            return gt[:, :].unsqueeze(1).to_broadcast([128, H, 64])

        def logsig(src, srcT):
            V("tensor_tensor", [srcT, bfb], [lft], out=lft[:, 0:6], in0=src, in1=bfb[:], op=ALU.add)
            ACT(lft[:, 6:12], lft[:, 0:6], AF.Exp, [lft], [lft], scale=-1.0)
            ACT(lft[:, 12:18], lft[:, 6:12], AF.Ln, [lft], [lft], bias=1.0)
            V("tensor_scalar", [lft], [lfo], out=lfo[:], in0=lft[:, 12:18], scalar1=-1.0, scalar2=None, op0=ALU.mult)

        def fox_block(zfill, N, bias3=None, biasT=None):
            a = nxt(A, "A")
            zfill(a)
            pt_ = rr(PT, "PT")
            if bias3 is None:
                ACT(pt_[:, 0:N], a[:, 0:N], AF.Exp, [a], [pt_])
            else:
                tmp = rr(ptmp, "p")
                V("tensor_tensor", [a, biasT], [tmp], out=bias3[0](tmp), in0=bias3[0](a), in1=bias3[1], op=ALU.add)
                ACT(pt_[:, 0:N], tmp[:, 0:N], AF.Exp, [tmp], [pt_])
            return pt_

        def sb_block(zfill, N, first):
            a = nxt(A, "A")
            zfill(a)
            ACT(etmp[:, 0:N], a[:, 0:N], AF.Exp, [a], [etmp])
            sp = rr(spb, "sp")
            ACT(sp[:, 0:N], etmp[:, 0:N], AF.Ln, [etmp], [sp], bias=1.0)
            b2 = nxt(Bk, "B")
            zfill(b2, last=False)
            MM(b2[:, 0:N], triSB[:], sp[:, 0:N], False, first, [triSB, sp], [b2])
            if not first:
                MM(b2[:, 0:N], negones[:], raccb[:, 0:N], False, True, [negones, raccb], [b2])
            pt_ = rr(PT, "PT")
            ACT(pt_[:, 0:N], b2[:, 0:N], AF.Exp, [b2], [pt_])
            if first:
                G("tensor_copy", [sp], [raccf], out=raccf[:, 0:N], in_=sp[:, 0:N])
            else:
                G("tensor_tensor", [sp, raccf], [raccf], out=raccf[:, 0:N], in0=raccf[:, 0:N], in1=sp[:, 0:N], op=ALU.add)
            G("tensor_copy", [raccf], [raccb], out=raccb[:, 0:N], in_=raccf[:, 0:N])
            return pt_


        def fox_attn(nb, zf_k, v_k, Ob, N=512):
            banks = {}

            def qk(i):
                a = nxt(A, "A")
                zf_k(i)(a)
                banks[i] = a
            qk(0)
            if nb > 1:
                qk(1)
            for i in range(nb):
                a = banks.pop(i)
                pt_ = rr(PT, "PT")
                ACT(pt_[:, 0:N], a[:, 0:N], AF.Exp, [a], [pt_])
                if i + 2 < nb:
                    qk(i + 2)
                lhs, lT = v_k(i)
                MM(Ob[0:65, 0:N], lhs, pt_[:, 0:N], i == 0, i == nb - 1, [lT, pt_], [Ob])

        def sb_attn(nb, zf_k, v_k, Ob, N=512):
            abank = {}
            spk = {}

            def qkA(i):
                a = nxt(A, "A")
                zf_k(i)(a)
                abank[i] = a

            def softplus(i):
                a = abank.pop(i)
                e = rr(etmps, "e")
                ACT(e[:, 0:N], a[:, 0:N], AF.Exp, [a], [e])
                sp = rr(spb, "sp")
                ACT(sp[:, 0:N], e[:, 0:N], AF.Ln, [e], [sp], bias=1.0)
                spk[i] = sp
            qkA(0)
            if nb > 1:
                qkA(1)
            softplus(0)
            if nb > 2:
                qkA(2)
            for i in range(nb):
                if i + 1 < nb:
                    softplus(i + 1)
                    if i + 3 < nb:
                        qkA(i + 3)
                sp = spk.pop(i)
                b2 = nxt(Bk, "B")
                zf_k(i)(b2, last=False)
                MM(b2[:, 0:N], triSB[:], sp[:, 0:N], False, i == 0, [triSB, sp], [b2])
                if i > 0:
                    MM(b2[:, 0:N], negones[:], raccb[:, 0:N], False, True, [negones, raccb], [b2])
                pt_ = rr(PT, "PT")
                ACT(pt_[:, 0:N], b2[:, 0:N], AF.Exp, [b2], [pt_])
                lhs, lT = v_k(i)
                MM(Ob[0:65, 0:N], lhs, pt_[:, 0:N], i == 0, i == nb - 1, [lT, pt_], [Ob])
                if i == 0:
                    G("tensor_copy", [sp], [raccf], out=raccf[:, 0:N], in_=sp[:, 0:N])
                else:
                    G("tensor_tensor", [sp, raccf], [raccf], out=raccf[:, 0:N], in0=raccf[:, 0:N], in1=sp[:, 0:N], op=ALU.add)
                if i + 1 < nb:
                    G("tensor_copy", [raccf], [raccb], out=raccb[:, 0:N], in_=raccf[:, 0:N])

        def finish_head(src_ap, srcT, ntile, has_den, hidx, o_n, tile0, copy=True):
            ncol = ntile * 128
            if copy:
                V("tensor_copy", [srcT], [OTs], out=OTs[0:65, 0:ncol], in_=src_ap)
            tb = nxt(Tk, "T")
            for i in range(ntile):
                TR(tb[:, i * 65:(i + 1) * 65], OTs[0:65, i * 128:(i + 1) * 128], ident_f[0:65, 0:65], [OTs, ident_f], [tb])
            t3 = tb[:, 0:ntile * 65].rearrange("p (i c) -> p i c", c=65)
            o3 = otok[:, 0:ntile, :]
            if has_den:
                V("reciprocal", [tb], [hs], out=hs[:, 40:40 + ntile].unsqueeze(2), in_=t3[:, :, 64:65])
                V("tensor_tensor", [tb, hs], [otok], out=o3, in0=t3[:, :, 0:64],
                  in1=hs[:, 40:40 + ntile].unsqueeze(2).to_broadcast([128, ntile, 64]), op=ALU.mult)
            else:
                V("tensor_copy", [tb], [otok], out=o3, in_=t3[:, :, 0:64])
            gain3 = gO[:, hidx * 64:(hidx + 1) * 64].unsqueeze(1).to_broadcast([128, ntile, 64])
            headnorm(o3, otok, ntile, gain3, [(o_n[:, tile0:tile0 + ntile, hidx * 64:(hidx + 1) * 64], o_n)])

        def merge_tile(o_n, ti, x_ap, rows, xm_row0, wout):
            tb = nxt(Tk, "T")
            tbv = tb[:].bitcast(BF16)
            for c in range(8):
                TR(tbv[:, c * 128:(c + 1) * 128], o_n[:, ti, c * 128:(c + 1) * 128], ident_b[:], [o_n, ident_b], [tb])
            h = rr(hT, "h")
            V("tensor_copy", [tb], [h], out=h[:].rearrange("p c t -> p (c t)"), in_=tbv[:, 0:1024])
            x = rr(xt, "x")
            if rows < 128:
                G("memset", [], [x], x[:], 0.0)
            DMA(x[0:rows, :], x_ap, [], [x])
            for half in range(2):
                bank = nxt(A, "A")
                proj(h, wout, half * 512, 512, bank)
                V("tensor_tensor", [bank, x], [x], out=x[:, half * 512:(half + 1) * 512], in0=bank[:, 0:512],
                  in1=x[:, half * 512:(half + 1) * 512], op=ALU.add)
            DMA(xmid[xm_row0:xm_row0 + 128, :], x[:], [x], [])

        stop = getattr(cfg, 'stop', 9)
        o_ns = sbt("o_ns", [128, 1, D], BF16)
        OTsamp = sbt("OTsamp", [65, 6, 128], F32)
        G("memset", [], [OTsamp], OTsamp[:], 1.0)

        def finish_sample(Ob, nh, has_den, h0):
            V("tensor_copy", [Ob], [OTsamp], out=OTsamp[0:65, 0:nh, 0:NTS],
              in_=Ob[0:65, 0:nh * NTS].rearrange("p (h c) -> p h c", h=nh))
            for h in range(nh):
                tb = nxt(Tk, "T")
                TR(tb[:, 0:65], OTsamp[0:65, h, :], ident_f[0:65, 0:65], [OTsamp, ident_f], [tb])
                o3 = otok[:, 0:1, :]
                if has_den:
                    V("reciprocal", [tb], [hs], out=hs[:, 40:41], in_=tb[:, 64:65])
                    V("tensor_scalar", [tb, hs], [otok], out=otok[:, 0, :], in0=tb[:, 0:64], scalar1=hs[:, 40:41],
                      scalar2=None, op0=ALU.mult)
                else:
                    V("tensor_copy", [tb], [otok], out=otok[:, 0, :], in_=tb[:, 0:64])
                hi = h0 + h
                gain3 = gO[:, hi * 64:(hi + 1) * 64].unsqueeze(1).to_broadcast([128, 1, 64])
                headnorm(o3, otok, 1, gain3, [(o_ns[:, 0:1, hi * 64:(hi + 1) * 64], o_ns)])

        def body():
            if stop <= 0:
                return
            with ExitStack() as st:
                NPAIR = NPG // 2
                ptb = sbt("ptb", [128, NBL * NPG], I32, st)
                idxV = sbt("idxV", [128, NBL * NPG], I32, st); idxK = sbt("idxK", [128, NBL * NPAIR], I32, st)
                iot = sbt("iot", [128, 2], F32, st)
                DMA(ptb[:], ptd.rearrange("o n -> (o n)").partition_broadcast(128), [], [ptb])
                G("iota", [], [iot], iot[:, 0:1], pattern=[[0, 1]], base=0, channel_multiplier=1, allow_small_or_imprecise_dtypes=True)
                G("iota", [], [iot], iot[:, 1:2], pattern=[[0, 1]], base=0, channel_multiplier=1, allow_small_or_imprecise_dtypes=True)
                V("tensor_scalar", [iot], [iot], out=iot[64:128, 1:2], in0=iot[64:128, 1:2], scalar1=-64.0, scalar2=None, op0=ALU.add)
                V("tensor_scalar", [ptb, iot], [idxV], out=idxV[:], in0=ptb[:], scalar1=128.0, scalar2=iot[:, 0:1],
                  op0=ALU.mult, op1=ALU.add)
                ptv = ptb[:].rearrange("p (m two) -> p m two", two=2)
                V("tensor_scalar", [ptb, iot], [idxK], out=idxK[0:64, :], in0=ptv[0:64, :, 0], scalar1=64.0,
                  scalar2=iot[0:64, 1:2], op0=ALU.mult, op1=ALU.add)
                V("tensor_scalar", [ptb, iot], [idxK], out=idxK[64:128, :], in0=ptv[64:128, :, 1], scalar1=64.0,
                  scalar2=iot[64:128, 1:2], op0=ALU.mult, op1=ALU.add)

                qdup = sbt("qdup", [128, 6, 128], BF16, st)
                qTf = sbt("qTf", [128, 6, 128], BF16, st); qTs = sbt("qTs", [128, 6, 128], BF16, st)
                kTnf = sbt("kTnf", [64, 6, 128], BF16, st); kTns = sbt("kTns", [64, 6, 128], BF16, st)
                vnf = sbt("vnf", [128, 385], BF16, st); vns = sbt("vns", [128, 385], BF16, st)
                G("memset", [], [vnf], vnf[:], 1.0)
                G("memset", [], [vns], vns[:], 1.0)
                mqTs = sbt("mqTs", [64, 4, 128], BF16, st)
                lfn = sbt("lfn", [128, 6], F32, st); dnew = sbt("dnew", [128, 6], F32, st)
                Cb = sbt("Cb", [128, NBL * 6], F32, st); DT = sbt("DT", [128, NBL * 6], F32, st)

                if stop <= 0.4:
                    return
                hS, xS = norm_T(xs, NTS, gA)
                if stop <= 0.6:
                    return
                hSk = sbt("hSk", [128, 8, 128], BF16, st)
                V("tensor_copy", [hS], [hSk], out=hSk[:], in_=hS[:])

                def tr_heads(src_fn, srcT, nh, rows_out, dst3, dstT):
                    tb = nxt(Tk, "T")
                    tbv = tb[:].bitcast(BF16)
                    for h in range(nh):
                        TR(tbv[0:rows_out, h * 128:(h + 1) * 128], src_fn(h), ident_b[:], [srcT, ident_b], [tb])
                    V("tensor_copy", [tb], [dstT], out=dst3,
                      in_=tbv[0:rows_out, 0:nh * 128].rearrange("p (h t) -> p h t", h=nh))

                load_w(w_in, 0, 0, 1158)
                bank = nxt(A, "A"); proj(hSk, wbuf, 0, 384, bank)
                headnorm(bank[:, 0:384].rearrange("p (h d) -> p h d", h=6), bank, 6, g3(gfq8, 6),
                         [(qdup[:, :, 0:64], qdup), (qdup[:, :, 64:128], qdup)])
                if stop <= 0.7:
                    return
                tr_heads(lambda h: qdup[:, h, :], qdup, 6, 128, qTf[:], qTf)
                if stop <= 0.8:
                    return
                bank = nxt(A, "A"); proj(hSk, wbuf, 384, 384, bank)
                pf = rr(ptmp, "p"); pb_ = rr(pbf, "pb")
                headnorm(bank[:, 0:384].rearrange("p (h d) -> p h d", h=6), bank, 6, g3(gfk, 6),
                         [(pf[:, 0:384].rearrange("p (h d) -> p h d", h=6), pf),
                          (pb_[:, 0:384].rearrange("p (h d) -> p h d", h=6), pb_)])
                DMA(s_fk, pf[0:NTS, 0:384], [pf], [])
                if stop <= 0.82:
                    return
                tr_heads(lambda h: pb_[:, h * 64:(h + 1) * 64], pb_, 6, 64, kTnf[:], kTnf)
                if stop <= 0.85:
                    return
                bank = nxt(A, "A"); proj(hSk, wbuf, 768, 390, bank)
                if stop <= 0.855:
                    return
                pf = rr(ptmp, "p")
                V("tensor_copy", [bank], [pf], out=pf[:, 0:384], in_=bank[:, 0:384])
                if stop <= 0.86:
                    return
                ACT(vnf[:, 0:384], bank[:, 0:384], AF.Copy, [bank], [vnf])
                if stop <= 0.865:
                    return
                DMA(s_fv, pf[0:NTS, 0:384], [pf], [])
                if stop <= 0.87:
                    return
                logsig(bank[:, 384:390], bank)
                if stop <= 0.88:
                    return
                DMA(s_lf, lfo[0:NTS, :], [lfo], [])
                V("tensor_copy", [lfo], [lfn], out=lfn[:], in_=lfo[:])
                if stop <= 0.9:
                    return
                tb = nxt(Tk, "T")
                MM(tb[:, 0:6], Mrev[:], lfn[:], True, True, [Mrev, lfn], [tb])
                for b in range(NBL):
                    MM(tb[:, 8 + b * 6:14 + b * 6], selB[:, b, :], lfn[:], True, True, [selB, lfn], [tb])
                V("tensor_copy", [tb], [dnew], out=dnew[:], in_=tb[:, 0:6])
                V("tensor_copy", [tb], [Cb], out=Cb[:], in_=tb[:, 8:8 + NBL * 6])
                if stop <= 0.95:
                    return
                load_w(w_in, 0, C_SQ, 1152)
                bank = nxt(A, "A"); proj(hSk, wbuf, 0, 384, bank)
                for half in range(2):
                    V("tensor_scalar", [bank], [qdup], out=qdup[:, :, half * 64:(half + 1) * 64],
                      in0=bank[:, 0:384].rearrange("p (h d) -> p h d", h=6), scalar1=0.125, scalar2=None, op0=ALU.mult)
                tr_heads(lambda h: qdup[:, h, :], qdup, 6, 128, qTs[:], qTs)
                bank = nxt(A, "A"); proj(hSk, wbuf, 384, 384, bank)
                pf = rr(ptmp, "p"); pb_ = rr(pbf, "pb")
                V("tensor_copy", [bank], [pf], out=pf[:, 0:384], in_=bank[:, 0:384])
                ACT(pb_[:, 0:384], bank[:, 0:384], AF.Copy, [bank], [pb_])
                DMA(s_sk, pf[0:NTS, 0:384], [pf], [])
                tr_heads(lambda h: pb_[:, h * 64:(h + 1) * 64], pb_, 6, 64, kTns[:], kTns)
                bank = nxt(A, "A"); proj(hSk, wbuf, 768, 384, bank)
                pf = rr(ptmp, "p")
                V("tensor_copy", [bank], [pf], out=pf[:, 0:384], in_=bank[:, 0:384])
                ACT(vns[:, 0:384], bank[:, 0:384], AF.Copy, [bank], [vns])
                DMA(s_sv, pf[0:NTS, 0:384], [pf], [])
                load_w(w_in, 0, C_MQ, 256)
                bank = nxt(A, "A"); proj(hSk, wbuf, 0, 256, bank)
                pb_ = rr(pbf, "pb")
                headnorm(bank[:, 0:256].rearrange("p (h d) -> p h d", h=4), bank, 4, g3(gmq8, 4),
                         [(pb_[:, 0:256].rearrange("p (h d) -> p h d", h=4), pb_)])
                tr_heads(lambda h: pb_[:, h * 64:(h + 1) * 64], pb_, 4, 64, mqTs[:], mqTs)

                if stop <= 1:
                    return
                KTb = [sbt("KTb%d" % i, [128, NBL, 768], BF16, st) for i in range(2)]
                Vb = [sbt("Vb%d" % i, [128, 2, NBL, 385], BF16, st) for i in range(2)]
                for i in range(2):
                    G("memset", [], [Vb[i]], Vb[i][:], 1.0)
                Qblk = sbt("Qblk", [128, NBL, 6, 16], BF16, st)
                Osb = sbt("Osb", [48, NBL, 385], F32, st)
                Otk = sbt("Otk", [128, 6, 65], F32, st)
                rt = sbt("rt", [128, NCS], F32, st)
                NBH = NBL * 6

                def idma(out, src2d, idx_ap, r, w, group):
                    return sc.dma("pool", lambda e: e.indirect_dma_start(
                        out=out, out_offset=None, in_=src2d, in_offset=bass.IndirectOffsetOnAxis(ap=idx_ap, axis=0)),
                        bl(r), bl(w), group=group)

                DT_all = sbt("DT_all", [128, NBH, NPG], F32, st)
                with ExitStack() as s2:
                    idxL = sbt("idxL", [128, NBL], I32, s2)
                    LFall = sbt("LFall", [128, NBL, 128, 6], F32, s2)
                    XT = sbt("XT", [128, NBH, NPG], F32, s2)
                    Dg = sbt("Dg", [128, NBH, NPG], F32, s2)
                    tot = sbt("tot", [128, NBH], F32, s2); car = sbt("car", [128, NBH], F32, s2)
                    flf_pg = flf.rearrange("(n r) h -> n (r h)", r=128)
                    grp = []
                    for b in range(NBL):
                        DMA(idxL[0:NPG, b:b + 1], ptd[0:1, b * NPG:(b + 1) * NPG].rearrange("o j -> j o"), [], [idxL])
                    for b in range(NBL):
                        idma(LFall[0:NPG, b, :, :].rearrange("p r h -> p (r h)"), flf_pg, idxL[0:NPG, b:b + 1], [idxL], [LFall], grp)
                    V("tensor_reduce", [LFall], [tot], out=tot[0:NPG, :].rearrange("p (b h) -> p b h", b=NBL),
                      in_=LFall[0:NPG, :, :, :].rearrange("p b r h -> p b h r"), axis=AX.X, op=ALU.add)
                    tb = nxt(Tk, "T")
                    MM(tb[0:NPG, 0:NBH], triU[0:NPG, 0:NPG], tot[0:NPG, :], True, True, [triU, tot], [tb])
                    V("tensor_tensor", [tb, Cb], [car], out=car[0:NPG, :], in0=tb[0:NPG, 0:NBH], in1=Cb[0:NPG, :], op=ALU.add)
                    V("tensor_tensor", [car, ident_f], [Dg], out=Dg[0:NPG, :, :],
                      in0=ident_f[0:NPG, 0:NPG].unsqueeze(1).to_broadcast([NPG, NBH, NPG]),
                      in1=car[0:NPG, :].unsqueeze(2).to_broadcast([NPG, NBH, NPG]), op=ALU.mult)
                    per = max(1, 512 // NPG)
                    for c0 in range(0, NBH, per):
                        n = min(per, NBH - c0)
                        tb = nxt(Tk, "T")
                        for i in range(n):
                            b, h = divmod(c0 + i, 6)
                            TR(tb[:, i * NPG:(i + 1) * NPG], LFall[0:NPG, b, :, h], ident_f[0:NPG, 0:NPG], [LFall, ident_f], [tb])
                        V("tensor_copy", [tb], [XT], out=XT[:, c0:c0 + n, :].rearrange("p m j -> p (m j)"), in_=tb[:, 0:n * NPG])
                    for c0 in range(0, NBH, per):
                        n = min(per, NBH - c0)
                        tb = nxt(Tk, "T")
                        MM(tb[:, 0:n * NPG], triU[:], XT[:, c0:c0 + n, :].rearrange("p m j -> p (m j)"), True, False, [triU, XT], [tb])
                        MM(tb[:, 0:n * NPG], ones_f[0:NPG, :], Dg[0:NPG, c0:c0 + n, :].rearrange("p m j -> p (m j)"), False, True,
                           [ones_f, Dg], [tb])
                        V("tensor_copy", [tb], [DT_all], out=DT_all[:, c0:c0 + n, :].rearrange("p m j -> p (m j)"), in_=tb[:, 0:n * NPG])

                for slot in range(2):
                    fox = slot == 0
                    kT2d, v2d = (fkT, fv) if fox else (skT, sv)
                    qT_, kTn_, vn_, mask = (qTf, kTnf, vnf, mnf) if fox else (qTs, kTns, vns, mns)
                    OB = [O[0], O[1], Bk[0], Bk[1]] if fox else [O[0], O[1], Tk[0], Tk[1]]
                    G("memset", [], [Qblk], Qblk[:], 0.0)
                    for half in range(2):
                        V("tensor_copy", [qT_], [Qblk], out=Qblk[half * 64:(half + 1) * 64, :, :, half * 8:(half + 1) * 8],
                          in_=qT_[half * 64:(half + 1) * 64, :, 0:NTS].rearrange("p h (b q) -> p b h q", b=NBL))

                    def pv_s(pt_, c0, b, rhs, rT, first, last):
                        MM(OB[b][0:48, 0:385], pt_[:, c0:c0 + 48], rhs, first, last, [pt_, rT], [OB[b]])

                    def zf_new(bank, last=True):
                        for h in range(6):
                            MM(bank[:, 0:NCS].rearrange("p (b h q) -> p b h q", b=NBL, h=6)[:, :, h, :], kTn_[0:64, h, :],
                               qT_[0:64, h, 0:NTS].rearrange("p (b q) -> p b q", b=NBL), h == 0, False, [kTn_, qT_], [bank])
                        MM(bank[:, 0:NCS], ident_b[:], mask[:], False, last, [ident_b, mask], [bank])
                    if fox:
                        a = nxt(A, "A")
                        zf_new(a)
                        tmp = rr(ptmp, "p")
                        V("tensor_tensor", [a, dnew], [tmp], out=tmp[:, 0:NCS].rearrange("p (b h q) -> p b h q", b=NBL, h=6),
                          in0=a[:, 0:NCS].rearrange("p (b h q) -> p b h q", b=NBL, h=6),
                          in1=dnew[:, :].unsqueeze(1).unsqueeze(3).to_broadcast([128, NBL, 6, T]), op=ALU.add)
                        pt_ = rr(PT, "PT")
                        ACT(pt_[:, 0:NCS], tmp[:, 0:NCS], AF.Exp, [tmp], [pt_])
                    else:
                        a = nxt(A, "A")
                        zf_new(a)
                        e = rr(etmps, "e")
                        ACT(e[:, 0:NCS], a[:, 0:NCS], AF.Exp, [a], [e])
                        sp = rr(spb, "sp")
                        ACT(sp[:, 0:NCS], e[:, 0:NCS], AF.Ln, [e], [sp], bias=1.0)
                        b2 = nxt(Bk, "B")
                        zf_new(b2, last=False)
                        MM(b2[:, 0:NCS], triSB[:], sp[:, 0:NCS], False, True, [triSB, sp], [b2])
                        pt_ = rr(PT, "PT")
                        ACT(pt_[:, 0:NCS], b2[:, 0:NCS], AF.Exp, [b2], [pt_])
                        V("tensor_copy", [sp], [raccf], out=raccf[:, 0:NCS], in_=sp[:, 0:NCS])
                        V("tensor_copy", [raccf], [raccb], out=raccb[:, 0:NCS], in_=raccf[:, 0:NCS])
                    for b in range(NBL):
                        pv_s(pt_, b * 48, b, vn_[:, 0:385], vn_, True, False)
                    N2 = 2 * NCS
                    for a_ in reversed(range(NPAIR)):
                        buf = a_ % 2
                        grp = []
                        for b in range(NBL):
                            idma(KTb[buf][:, b, :], kT2d, idxK[:, b * NPAIR + a_:b * NPAIR + a_ + 1], [idxK], [KTb[buf]], grp)
                        for pos in range(2):
                            for b in range(NBL):
                                ix = idxV[:, b * NPG + 2 * a_ + pos:b * NPG + 2 * a_ + pos + 1]
                                idma(Vb[buf][:, pos, b, 0:384], v2d, ix, [idxV], [Vb[buf]], grp)

                        def zf_pg(bank, last=True, buf=buf):
                            n = 0
                            ov = bank[:, 0:N2].rearrange("p (s b h q) -> p s b h q", s=2, b=NBL, h=6)
                            for b in range(NBL):
                                for h in range(6):
                                    MM(ov[:, :, b, h, :], KTb[buf][:, b, h * 128:(h + 1) * 128],
                                       Qblk[:, b, h, :].rearrange("p (s q) -> p s q", s=2), n == 0,
                                       last and n == NBH - 1, [KTb[buf], Qblk], [bank])
                                    n += 1
                        if fox:
                            a = nxt(A, "A")
                            zf_pg(a)
                            tmp = rr(ptmp, "p")
                            V("tensor_tensor", [a, DT_all], [tmp],
                              out=tmp[:, 0:N2].rearrange("p (s m q) -> p s m q", s=2, m=NBH),
                              in0=a[:, 0:N2].rearrange("p (s m q) -> p s m q", s=2, m=NBH),
                              in1=DT_all[:, :, 2 * a_:2 * a_ + 2].rearrange("p m s -> p s m").unsqueeze(3).to_broadcast([128, 2, NBH, T]),
                              op=ALU.add)
                            pt_ = rr(PT, "PT")
                            ACT(pt_[:, 0:N2], tmp[:, 0:N2], AF.Exp, [tmp], [pt_])
                        else:
                            a = nxt(A, "A")
                            zf_pg(a)
                            e = rr(etmps, "e")
                            ACT(e[:, 0:N2], a[:, 0:N2], AF.Exp, [a], [e])
                            sp = rr(spb, "sp")
                            ACT(sp[:, 0:N2], e[:, 0:N2], AF.Ln, [e], [sp], bias=1.0)
                            b2 = nxt(Bk, "B")
                            zf_pg(b2, last=False)
                            MM(b2[:, 0:N2], triSB[:], sp[:, 0:N2], False, False, [triSB, sp], [b2])
                            MM(b2[:, 0:NCS], negones[:], sp[:, NCS:N2], False, False, [negones, sp], [b2])
                            MM(b2[:, 0:NCS], negones[:], raccb[:, 0:NCS], False, False, [negones, raccb], [b2])
                            MM(b2[:, NCS:N2], negones[:], raccb[:, 0:NCS], False, True, [negones, raccb], [b2])
                            pt_ = rr(PT, "PT")
                            ACT(pt_[:, 0:N2], b2[:, 0:N2], AF.Exp, [b2], [pt_])
                            V("tensor_tensor", [sp], [rt], out=rt[:, 0:NCS], in0=sp[:, 0:NCS], in1=sp[:, NCS:N2], op=ALU.add)
                            V("tensor_tensor", [rt, raccf], [raccf], out=raccf[:, 0:NCS], in0=raccf[:, 0:NCS], in1=rt[:, 0:NCS], op=ALU.add)
                            V("tensor_copy", [raccf], [raccb], out=raccb[:, 0:NCS], in_=raccf[:, 0:NCS])
                        for pos in (1, 0):
                            for b in range(NBL):
                                pv_s(pt_, pos * NCS + b * 48, b, Vb[buf][:, pos, b, :], Vb[buf], False, a_ == 0 and pos == 0)
                    G("memset", [], [Otk], Otk[:], 1.0)
                    for b in range(NBL):
                        V("tensor_copy", [OB[b]], [Osb], out=Osb[0:48, b, :], in_=OB[b][0:48, 0:385])
                    for b in range(NBL):
                        for h in range(6):
                            DMA(Otk[b * T:(b + 1) * T, h, 0:64], Osb[h * T:(h + 1) * T, b, h * 64:(h + 1) * 64], [Osb], [Otk])
                            DMA(Otk[b * T:(b + 1) * T, h, 64:65], Osb[h * T:(h + 1) * T, b, 384:385], [Osb], [Otk])
                    h0 = 0 if fox else 6
                    o3 = otok[:, 0:4, :]
                    osrc = Otk[:, :, 0:64]
                    if fox:
                        V("reciprocal", [Otk], [hs], out=hs[:, 40:46].unsqueeze(2), in_=Otk[:, :, 64:65])
                        V("tensor_tensor", [Otk, hs], [Otk], out=Otk[:, :, 0:64], in0=Otk[:, :, 0:64],
                          in1=hs[:, 40:46].unsqueeze(2).to_broadcast([128, 6, 64]), op=ALU.mult)
                    headnorm(osrc, Otk, 6, gO[:, h0 * 64:(h0 + 6) * 64].rearrange("p (h d) -> p h d", h=6),
                             [(o_ns[:, 0, h0 * 64:(h0 + 6) * 64].rearrange("p (h d) -> p h d", h=6), o_ns)])

                if stop <= 2:
                    return
                cmk = sbt("cmk", [64, NBL, 4, 256], BF16, st); cmvb = sbt("cmvb", [128, NBL, 2, 256], BF16, st)
                for b in range(NBL):
                    stg = wst[b % 2]
                    DMA(stg[0:64, 0:1024], cmkT[b].rearrange("d h k -> d (h k)"), [], [stg])
                    G("tensor_copy", [stg], [cmk], out=cmk[0:64, b, :, :].rearrange("d h k -> d (h k)"), in_=stg[0:64, 0:1024])
                for b in range(NBL):
                    stg = wst[b % 2]
                    DMA(stg[:, 0:512].rearrange("p (kb c) -> p kb c", kb=2), cmv[b].rearrange("(kb p) c -> p kb c", p=128), [], [stg])
                    G("tensor_copy", [stg], [cmvb], out=cmvb[:, b, :, :].rearrange("p kb c -> p (kb c)"), in_=stg[:, 0:512])
                Ob = O[0]
                NM = 4 * NTS
                for kb in range(2):
                    def zf_m(bank, last=True, kb=kb):
                        n = 0
                        for b in range(NBL):
                            for h in range(4):
                                c0 = h * NTS + b * T
                                MM(bank[:, c0:c0 + T], cmk[0:64, b, h, kb * 128:(kb + 1) * 128], mqTs[0:64, h, b * T:(b + 1) * T],
                                   n == 0, last and n == NBL * 4 - 1, [cmk, mqTs], [bank])
                                n += 1
                    pt_ = fox_block(zf_m, NM)
                    n = 0
                    for b in range(NBL):
                        for h in range(4):
                            c0 = h * NTS + b * T
                            MM(Ob[0:64, c0:c0 + T], cmvb[:, b, kb, h * 64:(h + 1) * 64], pt_[:, c0:c0 + T],
                               kb == 0 and n == 0, False, [cmvb, pt_], [Ob])
                            n += 1
                    MM(Ob[64:65, 0:NM], onecol[:, 0:1], pt_[:, 0:NM], kb == 0, kb == 1, [onecol, pt_], [Ob])
                finish_sample(Ob, 4, True, 12)
            sc.barrier()

            if stop <= 3:
                return
            with ExitStack() as st:
                qT = sbt("qT", [128, 6, S], BF16, st); kT = sbt("kT", [128, 6, S], BF16, st)
                vT = sbt("vT", [128, NT, 6, 65], BF16, st); o_n = sbt("o_n", [128, NT, D], BF16, st)
                mkT = sbt("mkT", [64, 4, 256], BF16, st); mvb = sbt("mvb", [128, 2, 4, 65], BF16, st)
                G("memset", [], [vT], vT[:], 1.0)
                G("memset", [], [mvb], mvb[:], 1.0)

                def tr_to(src_fn, srcT, nh, rows, dstT, tt):
                    tb = nxt(Tk, "T")
                    tbv = tb[:].bitcast(BF16)
                    for h in range(nh):
                        TR(tbv[0:rows, h * 128:(h + 1) * 128], src_fn(h), ident_b[:], [srcT, ident_b], [tb])
                    V("tensor_copy", [tb], [dstT], out=dstT[0:rows, 0:nh, tt * 128:(tt + 1) * 128],
                      in_=tbv[0:rows, 0:nh * 128].rearrange("p (h t) -> p h t", h=nh))

                def v3(ap384):
                    return ap384.rearrange("p (h d) -> p h d", h=6)

                for seq in range(NSEQ):
                    r0 = seq * S

                    def xrows(tt):
                        return xp[r0 + tt * 128:r0 + (tt + 1) * 128, :]
                    load_w(w_in, 0, 0, 1158)
                    G("memset", [], [carry], carry[:], 0.0)
                    h, _ = norm_T(xrows(0), 128, gA)
                    for tt in range(NT):
                        rows = slice(r0 + tt * 128, r0 + (tt + 1) * 128)
                        b1 = nxt(A, "A"); proj(h, wbuf, 0, 384, b1)
                        b2 = nxt(A, "A"); proj(h, wbuf, 384, 384, b2)
                        b3 = nxt(Bk, "B"); proj(h, wbuf, 768, 390, b3)
                        if tt + 1 < NT:
                            h, _ = norm_T(xrows(tt + 1), 128, gA)
                        headnorm(v3(b1[:, 0:384]), b1, 6, g3(gfq8, 6), [(qaug[:, :, 0:64], qaug)])
                        pf = rr(ptmp, "p")
                        headnorm(v3(b2[:, 0:384]), b2, 6, g3(gfk, 6), [(v3(pf[:, 0:384]), pf), (kaug[:, :, 0:64], kaug)])
                        DMA(p_fk[rows, :], pf[:, 0:384], [pf], [])
                        pf = rr(ptmp, "p")
                        V("tensor_copy", [b3], [pf], out=pf[:, 0:384], in_=b3[:, 0:384])
                        ACT(vT[:, tt, :, 0:64], v3(b3[:, 0:384]), AF.Copy, [b3], [vT])
                        DMA(p_fv[rows, :], pf[:, 0:384], [pf], [])
                        logsig(b3[:, 384:390], b3)
                        DMA(p_lf[rows, :], lfo[:], [lfo], [])
                        tb = nxt(Tk, "T")
                        MM(tb[:, 0:6], triL[:], lfo[:], True, True, [triL, lfo], [tb])
                        MM(tb[:, 8:14], ones_f[:], lfo[:], True, True, [ones_f, lfo], [tb])
                        V("tensor_tensor", [tb, carry], [cc], out=cc[:], in0=tb[:, 0:6], in1=carry[:], op=ALU.add)
                        V("tensor_tensor", [tb, carry], [carry], out=carry[:], in0=tb[:, 8:14], in1=carry[:], op=ALU.add)
                        V("tensor_copy", [cc], [chi], out=chi[:], in_=cc[:])
                        V("tensor_tensor", [cc, chi], [clo], out=clo[:], in0=cc[:], in1=chi[:], op=ALU.subtract)
                        V("tensor_copy", [chi], [qaug], out=qaug[:, :, 64], in_=chi[:])
                        V("tensor_copy", [clo], [qaug], out=qaug[:, :, 65], in_=clo[:])
                        V("tensor_scalar", [cc], [kaug], out=kaug[:, :, 66], in0=cc[:], scalar1=-1.0, scalar2=None, op0=ALU.mult)
                        V("tensor_scalar", [clo], [kaug], out=kaug[:, :, 67], in0=clo[:], scalar1=-1.0, scalar2=None, op0=ALU.mult)
                        tr_to(lambda hh: qaug[:, hh, :], qaug, 6, 68, qT, tt)
                        tr_to(lambda hh: kaug[:, hh, :], kaug, 6, 68, kT, tt)
                    for r in range(NR):
                        for hh in range(6):
                            Ob = nxt(O, "O")
                            nb = 4 * r + 4

                            def zf_k(i, hh=hh, r=r):
                                kb = i

                                def zf(bank, last=True):
                                    diag = kb >= 4 * r
                                    MM(bank[:, 0:512], kT[0:68, hh, kb * 128:(kb + 1) * 128], qT[0:68, hh, r * 512:(r + 1) * 512],
                                       True, last and not diag, [kT, qT], [bank])
                                    if diag:
                                        MM(bank[:, 0:512], ident_b[:], mfox[:, kb - 4 * r, :], False, last, [ident_b, mfox], [bank])
                                return zf
                            fox_attn(nb, zf_k, lambda i, hh=hh: (vT[:, i, hh, :], vT), Ob)
                            finish_head(Ob[0:65, 0:512], Ob, 4, True, hh, o_n, 4 * r)
                    load_w(w_in, 0, C_SQ, 1152)
                    h, _ = norm_T(xrows(0), 128, gA)
                    for tt in range(NT):
                        rows = slice(r0 + tt * 128, r0 + (tt + 1) * 128)
                        b1 = nxt(A, "A"); proj(h, wbuf, 0, 384, b1)
                        b2 = nxt(A, "A"); proj(h, wbuf, 384, 384, b2)
                        b3 = nxt(Bk, "B"); proj(h, wbuf, 768, 384, b3)
                        if tt + 1 < NT:
                            h, _ = norm_T(xrows(tt + 1), 128, gA)
                        pb_ = rr(pbf, "pb")
                        ACT(pb_[:, 0:384], b1[:, 0:384], AF.Copy, [b1], [pb_], scale=0.125)
                        tr_to(lambda hh: pb_[:, hh * 64:(hh + 1) * 64], pb_, 6, 64, qT, tt)
                        pf = rr(ptmp, "p"); pb_ = rr(pbf, "pb")
                        V("tensor_copy", [b2], [pf], out=pf[:, 0:384], in_=b2[:, 0:384])
                        ACT(pb_[:, 0:384], b2[:, 0:384], AF.Copy, [b2], [pb_])
                        DMA(p_sk[rows, :], pf[:, 0:384], [pf], [])
                        tr_to(lambda hh: pb_[:, hh * 64:(hh + 1) * 64], pb_, 6, 64, kT, tt)
                        pf = rr(ptmp, "p")
                        V("tensor_copy", [b3], [pf], out=pf[:, 0:384], in_=b3[:, 0:384])
                        ACT(vT[:, tt, :, 0:64], v3(b3[:, 0:384]), AF.Copy, [b3], [vT])
                        DMA(p_sv[rows, :], pf[:, 0:384], [pf], [])
                    for r in range(NR):
                        for hh in range(6):
                            Ob = nxt(O, "O")
                            nb = 4 * r + 4

                            def zf_k(i, hh=hh, r=r, nb=nb):
                                kb = nb - 1 - i

                                def zf(bank, last=True):
                                    diag = kb >= 4 * r
                                    MM(bank[:, 0:512], kT[0:64, hh, kb * 128:(kb + 1) * 128], qT[0:64, hh, r * 512:(r + 1) * 512],
                                       True, last and not diag, [kT, qT], [bank])
                                    if diag:
                                        MM(bank[:, 0:512], ident_b[:], msb[:, kb - 4 * r, :], False, last, [ident_b, msb], [bank])
                                return zf
                            sb_attn(nb, zf_k, lambda i, hh=hh, nb=nb: (vT[:, nb - 1 - i, hh, :], vT), Ob)
                            finish_head(Ob[0:65, 0:512], Ob, 4, False, 6 + hh, o_n, 4 * r)
                    load_w(w_mem, 0, 0, 512)
                    DMA(gX[:], bcast_row(g_mem_in, D), [], [gX])
                    for mt in range(2):
                        rows = slice(seq * 256 + mt * 128, seq * 256 + (mt + 1) * 128)
                        h, _ = norm_T(memp[rows, :], 128, gX)
                        bank = nxt(A, "A"); proj(h, wbuf, 0, 512, bank)
                        pf = rr(ptmp, "p"); pb_ = rr(pbf, "pb")
                        headnorm(bank[:, 0:256].rearrange("p (h d) -> p h d", h=4), bank, 4, g3(gmk, 4),
                                 [(pf[:, 0:256].rearrange("p (h d) -> p h d", h=4), pf),
                                  (pb_[:, 0:256].rearrange("p (h d) -> p h d", h=4), pb_)])
                        DMA(p_mk[rows, :], pf[:, 0:256], [pf], [])
                        tb = nxt(Tk, "T")
                        tbv = tb[:].bitcast(BF16)
                        for hh in range(4):
                            TR(tbv[0:64, hh * 128:(hh + 1) * 128], pb_[:, hh * 64:(hh + 1) * 64], ident_b[:], [pb_, ident_b], [tb])
                        V("tensor_copy", [tb], [mkT], out=mkT[0:64, :, mt * 128:(mt + 1) * 128],
                          in_=tbv[0:64, 0:512].rearrange("p (h t) -> p h t", h=4))
                        pf = rr(ptmp, "p")
                        V("tensor_copy", [bank], [pf], out=pf[:, 0:256], in_=bank[:, 256:512])
                        ACT(mvb[:, mt, :, 0:64], bank[:, 256:512].rearrange("p (h d) -> p h d", h=4), AF.Copy, [bank], [mvb])
                        DMA(p_mv[rows, :], pf[:, 0:256], [pf], [])
                    load_w(w_in, 0, C_MQ, 256)
                    h, _ = norm_T(xrows(0), 128, gA)
                    for tt in range(NT):
                        bank = nxt(A, "A"); proj(h, wbuf, 0, 256, bank)
                        if tt + 1 < NT:
                            h, _ = norm_T(xrows(tt + 1), 128, gA)
                        pb_ = rr(pbf, "pb")
                        headnorm(bank[:, 0:256].rearrange("p (h d) -> p h d", h=4), bank, 4, g3(gmq8, 4),
                                 [(pb_[:, 0:256].rearrange("p (h d) -> p h d", h=4), pb_)])
                        tr_to(lambda hh: pb_[:, hh * 64:(hh + 1) * 64], pb_, 4, 64, qT, tt)
                    for r in range(NR):
                        for hh in range(4):
                            Ob = nxt(O, "O")

                            def zf_k(i, hh=hh, r=r):
                                kb = i

                                def zf(bank, last=True):
                                    MM(bank[:, 0:512], mkT[0:64, hh, kb * 128:(kb + 1) * 128], qT[0:64, hh, r * 512:(r + 1) * 512],
                                       True, last, [mkT, qT], [bank])
                                return zf
                            fox_attn(2, zf_k, lambda i, hh=hh: (mvb[:, i, hh, :], mvb), Ob)
                            finish_head(Ob[0:65, 0:512], Ob, 4, True, 12 + hh, o_n, 4 * r)
                    load_w(w_out, 0, 0, 1024)
                    for tt in range(NT):
                        merge_tile(o_n, tt, xrows(tt), 128, r0 + tt * 128, wbuf)
                merge_tile(o_ns, 0, xs, NTS, NSEQ * S, wbuf)
            sc.barrier()
            esA.close()

            if stop <= 4:
                return
            with ExitStack() as st:
                wup = sbt("wup", [128, 8, DFF], BF16, st); wdn = sbt("wdn", [128, NFC, D], BF16, st)
                uT = sbt("uT", [128, NFC, 256], BF16, st)
                xm = [sbt("xm0", [128, D], F32, st), sbt("xm1", [128, D], F32, st)]
                rl = [sbt("rl0", [128, 256], BF16, st), sbt("rl1", [128, 256], BF16, st)]
                DMA(gX[:], bcast_row(g_ffn, D), [], [gX])
                for kc in range(8):
                    for c0 in range(0, DFF, 1024):
                        n = min(1024, DFF - c0)
                        stg = wst[(kc + c0 // 1024) % 2]
                        DMA(stg[:, 0:n], w_up[kc * 128:(kc + 1) * 128, c0:c0 + n], [], [stg])
                        G("tensor_copy", [stg], [wup], out=wup[:, kc, c0:c0 + n], in_=stg[:, 0:n])
                for fc in range(NFC):
                    stg = wst[fc % 2]
                    DMA(stg[:, 0:D], w_down[fc * 128:(fc + 1) * 128, :], [], [stg])
                    G("tensor_copy", [stg], [wdn], out=wdn[:, fc, :], in_=stg[:, 0:D])
                groups = [(g * 256, 2) for g in range(NSEQ * S // 256)] + [(NSEQ * S, 1)]
                for (row0, ntile) in groups:
                    ntok = ntile * 128
                    hs_ = []
                    for i in range(ntile):
                        DMA(xm[i][:], xmid[row0 + i * 128:row0 + (i + 1) * 128, :], [], [xm[i]])
                        h, _ = norm_T(None, 128, gX, src_t=xm[i])
                        hs_.append(h)
                    for fc in range(NFC):
                        bank = nxt(A, "A")
                        for i in range(ntile):
                            for kc in range(8):
                                MM(bank[:, i * 128:(i + 1) * 128], wup[:, kc, fc * 128:(fc + 1) * 128], hs_[i][:, kc, :],
                                   kc == 0, kc == 7, [wup, hs_[i]], [bank])
                        r_ = rl[fc % 2]
                        ACT(r_[:, 0:ntok], bank[:, 0:ntok], AF.Relu, [bank], [r_])
                        V("tensor_tensor", [r_], [uT], out=uT[:, fc, 0:ntok], in0=r_[:, 0:ntok], in1=r_[:, 0:ntok], op=ALU.mult)
                    for i in range(ntile):
                        for half in range(2):
                            bank = nxt(Bk, "B")
                            for fc in range(NFC):
                                MM(bank[:, 0:512], uT[:, fc, i * 128:(i + 1) * 128], wdn[:, fc, half * 512:(half + 1) * 512],
                                   fc == 0, fc == NFC - 1, [uT, wdn], [bank])
                            yt = rr(ptmp, "p")
                            V("tensor_tensor", [bank, xm[i]], [yt], out=yt[:, 0:512], in0=bank[:, 0:512],
                              in1=xm[i][:, half * 512:(half + 1) * 512], op=ALU.add)
                            rw = row0 + i * 128
                            if rw < NSEQ * S:
                                DMA(yp[rw:rw + 128, half * 512:(half + 1) * 512], yt[:, 0:512], [yt], [])
                            else:
                                DMA(ys[0:NTS, half * 512:(half + 1) * 512], yt[0:NTS, 0:512], [yt], [])
        try:
            body()
        except _Stop:
            pass
        esA.close()
        sc.finish("sp")
    return nc


def _prep(inputs, cfg, ncores=8):
    f = lambda a: np.ascontiguousarray(np.asarray(a))
    NSEQ, S, NPG, NBL, T = cfg.NSEQ, cfg.S, cfg.NPG, cfg.NBL, cfg.T
    fk = np.asarray(inputs["cache_fox_k"])[0]; sk = np.asarray(inputs["cache_sb_k"])[0]
    npool = fk.shape[0]
    shared = {
        "fkT": f(fk.transpose(0, 3, 2, 1)).reshape(npool * 64, 768),
        "skT": f(sk.transpose(0, 3, 2, 1)).reshape(npool * 64, 768),
        "fv": f(np.asarray(inputs["cache_fox_v"])[0]).reshape(npool * 128, 384),
        "sv": f(np.asarray(inputs["cache_sb_v"])[0]).reshape(npool * 128, 384),
        "flf": f(np.asarray(inputs["cache_fox_logf"])[0]).reshape(npool * 128, 6),
    }
    for nm, key in (("g_attn", "g_attn"), ("b_f", "b_forget"), ("g_fq", "g_fox_q"), ("g_fk", "g_fox_k"),
                    ("g_mem_in", "g_mem_in"), ("g_mq", "g_mem_q"), ("g_mk", "g_mem_k"), ("g_out", "g_out"),
                    ("g_ffn", "g_ffn")):
        shared[nm] = f(np.asarray(inputs[key])[0]).reshape(1, -1)
    for nm, key in (("w_in", "w_in"), ("w_mem", "w_mem_kv"), ("w_out", "w_out"), ("w_up", "w_up"), ("w_down", "w_down")):
        shared[nm] = f(np.asarray(inputs[key])[0])
    xp = np.asarray(inputs["x_prompt"]); xsm = np.asarray(inputs["x_sample"]); mem = np.asarray(inputs["mem_prompt"])
    cmk = np.asarray(inputs["cache_mem_k"])[0]; cmv = np.asarray(inputs["cache_mem_v"])[0]
    pt = np.asarray(inputs["page_table"]).astype(np.int32)
    maps = []
    for c in range(ncores):
        m = dict(shared)
        m["xp"] = f(xp[c * NSEQ:(c + 1) * NSEQ]).reshape(NSEQ * S, D)
        m["memp"] = f(mem[c * NSEQ:(c + 1) * NSEQ]).reshape(NSEQ * 256, D)
        m["xs"] = f(xsm[c * NBL:(c + 1) * NBL]).reshape(NBL * T, D)
        m["cmkT"] = f(cmk[c * NBL:(c + 1) * NBL].transpose(0, 3, 2, 1))
        m["cmv"] = f(cmv[c * NBL:(c + 1) * NBL]).reshape(NBL, 256, 256)
        m["pt"] = f(pt[c * NBL:(c + 1) * NBL]).reshape(1, NBL * NPG)
        maps.append(m)
    return maps


def _assemble(res, cfg, ncores=8):
    NSEQ, S, NBL, T = cfg.NSEQ, cfg.S, cfg.NBL, cfg.T
    cat = lambda k: np.concatenate([np.asarray(r[k]) for r in res], axis=0)
    B = ncores * NSEQ; Bs = ncores * NBL
    return (cat("yp").reshape(B, S, D), cat("ys").reshape(Bs, T, D),
            cat("p_fk").reshape(1, B, S, 6, 64), cat("p_fv").reshape(1, B, S, 6, 64), cat("p_lf").reshape(1, B, S, 6),
            cat("p_sk").reshape(1, B, S, 6, 64), cat("p_sv").reshape(1, B, S, 6, 64),
            cat("p_mk").reshape(1, B, 256, 4, 64), cat("p_mv").reshape(1, B, 256, 4, 64),
            cat("s_fk").reshape(1, Bs, T, 6, 64), cat("s_fv").reshape(1, Bs, T, 6, 64), cat("s_lf").reshape(1, Bs, T, 6),
            cat("s_sk").reshape(1, Bs, T, 6, 64), cat("s_sv").reshape(1, Bs, T, 6, 64))


def run(inputs, cfg):
    nc = build(cfg)
    maps = _prep(inputs, cfg)
    res = run_bass_kernel_spmd(nc, maps, core_ids=list(range(8)))
    return _assemble(res.results, cfg)


def kernel(**inputs):
    return run(inputs, Cfg())
```
